# Optimizing a Trainium2 kernel written in Bass

```python
import math
import jax, jax.numpy as jnp
from jax import lax
import numpy as np

D_MODEL = 1024
BATCH = 16
SEQ = 2048
DEPTH = 2

N_META = 16
HEAD_DIM = 64
N_Q_HEADS = 8
N_KV_HEADS = 2
GQA_GROUP = N_Q_HEADS // N_KV_HEADS
ATTN_WIDTH = N_Q_HEADS * HEAD_DIM
KV_WIDTH = N_KV_HEADS * HEAD_DIM
WINDOW = 128
BLOCK = 128
CONV_WIDTH = D_MODEL // 2
CONV_TAPS = 31
N_BRANCHES = 2
REL_BUCKETS = 32
REL_MAX_DIST = 128
D_FF = 2816
FFN_TAPS = 3
LN_EPS = 1e-5
DEEPNORM_ALPHA = (2.0 * DEPTH) ** 0.25
DEEPNORM_BETA = (8.0 * DEPTH) ** -0.25
IN_COLS = ATTN_WIDTH + 2 * KV_WIDTH + 2 * CONV_WIDTH + N_BRANCHES * D_MODEL

kernel_name = "hybrid_conformer_swa_sink_gated"


def layer_norm(x, g, b):
    xf = x.astype(jnp.float32)
    mu = xf.mean(-1, keepdims=True)
    var = jnp.square(xf - mu).mean(-1, keepdims=True)
    return ((xf - mu) * lax.rsqrt(var + LN_EPS) * g + b).astype(x.dtype)


def causal_dwconv(x, w, b):
    taps, c = w.shape
    y = lax.conv_general_dilated(x, w[:, None, :].astype(x.dtype), window_strides=(1,),
                                 padding=[(taps - 1, 0)],
                                 dimension_numbers=('NWC', 'WIO', 'NWC'),
                                 feature_group_count=c)
    return y + b.astype(x.dtype)


def t5_bucket(d):
    n = jnp.maximum(d, 0)
    max_exact = REL_BUCKETS // 2
    nf = jnp.maximum(n, 1).astype(jnp.float32)
    large = max_exact + (jnp.log(nf / max_exact) / math.log(REL_MAX_DIST / max_exact)
                         * (REL_BUCKETS - max_exact)).astype(jnp.int32)
    large = jnp.minimum(large, REL_BUCKETS - 1)
    return jnp.where(n < max_exact, n, large)


def swa_sink_attention(q, k, v, sinks, rel_bias):
    B, T = q.shape[0], q.shape[1]
    pad = BLOCK - N_META
    nblk = (T + pad) // BLOCK
    padw = ((0, 0), (pad, 0), (0, 0), (0, 0))
    qp = jnp.pad(q, padw).reshape(B, nblk, BLOCK, N_KV_HEADS, GQA_GROUP, HEAD_DIM)
    kp = jnp.pad(k, padw).reshape(B, nblk, BLOCK, N_KV_HEADS, HEAD_DIM)
    vp = jnp.pad(v, padw).reshape(B, nblk, BLOCK, N_KV_HEADS, HEAD_DIM)
    shift = ((0, 0), (1, 0), (0, 0), (0, 0), (0, 0))
    kb = jnp.concatenate([jnp.pad(kp[:, :-1], shift), kp], axis=2)
    vb = jnp.concatenate([jnp.pad(vp[:, :-1], shift), vp], axis=2)
    k_meta, v_meta = k[:, :N_META], v[:, :N_META]

    blk = jnp.arange(nblk)[:, None]
    q_pos = blk * BLOCK + jnp.arange(BLOCK)[None, :] - pad
    k_pos = (blk - 1) * BLOCK + jnp.arange(2 * BLOCK)[None, :] - pad
    d_band = q_pos[:, :, None] - k_pos[:, None, :]
    mask_band = (d_band >= 0) & (d_band < WINDOW) & (k_pos >= N_META)[:, None, :]
    d_meta = q_pos[:, :, None] - jnp.arange(N_META)[None, None, :]
    mask_meta = d_meta >= 0

    rb = rel_bias.astype(jnp.float32)
    def bias_of(d):
        bsh = jnp.moveaxis(rb[t5_bucket(d)], -1, 0)
        return bsh.reshape((N_KV_HEADS, GQA_GROUP) + d.shape)

    scale = HEAD_DIM ** -0.5
    neg = jnp.finfo(jnp.float32).min
    s_band = jnp.einsum('bnqhgd,bnkhd->bhgnqk', qp, kb).astype(jnp.float32)
    s_band = jnp.where(mask_band, s_band * scale + bias_of(d_band), neg)
    s_meta = jnp.einsum('bnqhgd,bmhd->bhgnqm', qp, k_meta).astype(jnp.float32)
    s_meta = jnp.where(mask_meta, s_meta * scale + bias_of(d_meta), neg)

    sink = sinks.astype(jnp.float32).reshape(N_KV_HEADS, GQA_GROUP)[None, :, :, None, None, None]
    m = jnp.maximum(jnp.maximum(s_band.max(-1, keepdims=True), s_meta.max(-1, keepdims=True)), sink)
    p_band = jnp.exp(s_band - m)
    p_meta = jnp.exp(s_meta - m)
    denom = p_band.sum(-1, keepdims=True) + p_meta.sum(-1, keepdims=True) + jnp.exp(sink - m)
    p_band = (p_band / denom).astype(v.dtype)
    p_meta = (p_meta / denom).astype(v.dtype)
    o = (jnp.einsum('bhgnqk,bnkhd->bnqhgd', p_band, vb)
         + jnp.einsum('bhgnqm,bmhd->bnqhgd', p_meta, v_meta))
    o = o.reshape(B, nblk * BLOCK, ATTN_WIDTH)
    return o[:, pad:]


def hybrid_layer(h, rel_bias, w_in, b_in, attn_sinks, w_attn_proj, conv_dw, conv_dw_b,
                 conv_ln_g, conv_ln_b, w_conv_proj, w_out, ln1_g, ln1_b,
                 ffn_w_up, ffn_dw, ffn_dw_b, ffn_w_down, ln2_g, ln2_b):
    B, T, _ = h.shape
    z = h @ w_in + b_in
    q, k, v, c_in, gates = jnp.split(
        z, [ATTN_WIDTH, ATTN_WIDTH + KV_WIDTH, ATTN_WIDTH + 2 * KV_WIDTH,
            ATTN_WIDTH + 2 * KV_WIDTH + 2 * CONV_WIDTH], axis=-1)
    a = swa_sink_attention(q.reshape(B, T, N_Q_HEADS, HEAD_DIM),
                           k.reshape(B, T, N_KV_HEADS, HEAD_DIM),
                           v.reshape(B, T, N_KV_HEADS, HEAD_DIM), attn_sinks, rel_bias)
    y_attn = a @ w_attn_proj
    c_val, c_gate = jnp.split(c_in, 2, axis=-1)
    c = c_val * jax.nn.sigmoid(c_gate)
    c = causal_dwconv(c, conv_dw, conv_dw_b)
    c = jax.nn.silu(layer_norm(c, conv_ln_g, conv_ln_b))
    y_conv = c @ w_conv_proj
    g_attn, g_conv = jnp.split(gates, 2, axis=-1)
    mixed = jax.nn.sigmoid(g_attn) * y_attn + jax.nn.sigmoid(g_conv) * y_conv
    h = layer_norm(DEEPNORM_ALPHA * h + mixed @ w_out, ln1_g, ln1_b)
    up = causal_dwconv(h @ ffn_w_up, ffn_dw, ffn_dw_b)
    u, g = jnp.split(up, 2, axis=-1)
    f = (jax.nn.gelu(g, approximate=False) * u) @ ffn_w_down
    return layer_norm(DEEPNORM_ALPHA * h + f, ln2_g, ln2_b)


def setup_inputs(seed: int = 0) -> dict:
    key = jax.random.key(seed)
    ks = jax.random.split(key, 24)
    def nrm(k, shape, s):
        return jax.random.normal(k, shape, jnp.float32) * s
    col_scale = jnp.ones((IN_COLS,), jnp.float32).at[
        ATTN_WIDTH + KV_WIDTH:ATTN_WIDTH + 2 * KV_WIDTH].set(DEEPNORM_BETA)
    return {
        "x": nrm(ks[0], (BATCH, SEQ, D_MODEL), 1.0),
        "meta_tokens": nrm(ks[1], (N_META, D_MODEL), 1.0),
        "in_ln_g": 1.0 + nrm(ks[2], (D_MODEL,), 0.02),
        "in_ln_b": nrm(ks[3], (D_MODEL,), 0.02),
        "rel_bias": nrm(ks[4], (REL_BUCKETS, N_Q_HEADS), 0.1),
        "w_in": nrm(ks[5], (DEPTH, D_MODEL, IN_COLS), D_MODEL ** -0.5) * col_scale,
        "b_in": nrm(ks[6], (DEPTH, IN_COLS), 0.01),
        "attn_sinks": nrm(ks[7], (DEPTH, N_Q_HEADS), 0.5),
        "w_attn_proj": nrm(ks[8], (DEPTH, ATTN_WIDTH, D_MODEL), ATTN_WIDTH ** -0.5 * DEEPNORM_BETA),
        "conv_dw": nrm(ks[9], (DEPTH, CONV_TAPS, CONV_WIDTH), CONV_TAPS ** -0.5),
        "conv_dw_b": nrm(ks[10], (DEPTH, CONV_WIDTH), 0.01),
        "conv_ln_g": 1.0 + nrm(ks[11], (DEPTH, CONV_WIDTH), 0.02),
        "conv_ln_b": nrm(ks[12], (DEPTH, CONV_WIDTH), 0.02),
        "w_conv_proj": nrm(ks[13], (DEPTH, CONV_WIDTH, D_MODEL), CONV_WIDTH ** -0.5 * DEEPNORM_BETA),
        "w_out": nrm(ks[14], (DEPTH, D_MODEL, D_MODEL), D_MODEL ** -0.5 * DEEPNORM_BETA),
        "ln1_g": 1.0 + nrm(ks[15], (DEPTH, D_MODEL), 0.02),
        "ln1_b": nrm(ks[16], (DEPTH, D_MODEL), 0.02),
        "ffn_w_up": nrm(ks[17], (DEPTH, D_MODEL, 2 * D_FF), D_MODEL ** -0.5 * DEEPNORM_BETA),
        "ffn_dw": nrm(ks[18], (DEPTH, FFN_TAPS, 2 * D_FF), FFN_TAPS ** -0.5),
        "ffn_dw_b": nrm(ks[19], (DEPTH, 2 * D_FF), 0.01),
        "ffn_w_down": nrm(ks[20], (DEPTH, D_FF, D_MODEL), D_FF ** -0.5 * DEEPNORM_BETA),
        "ln2_g": 1.0 + nrm(ks[21], (DEPTH, D_MODEL), 0.02),
        "ln2_b": nrm(ks[22], (DEPTH, D_MODEL), 0.02),
    }


def reference(x, meta_tokens, in_ln_g, in_ln_b, rel_bias, w_in, b_in, attn_sinks,
              w_attn_proj, conv_dw, conv_dw_b, conv_ln_g, conv_ln_b, w_conv_proj, w_out,
              ln1_g, ln1_b, ffn_w_up, ffn_dw, ffn_dw_b, ffn_w_down, ln2_g, ln2_b):
    B = x.shape[0]
    meta = jnp.broadcast_to(meta_tokens[None].astype(x.dtype), (B, N_META, D_MODEL))
    h = jnp.concatenate([meta, x], axis=1)
    h = layer_norm(h, in_ln_g, in_ln_b)
    for i in range(DEPTH):
        h = hybrid_layer(h, rel_bias, w_in[i], b_in[i], attn_sinks[i], w_attn_proj[i],
                         conv_dw[i], conv_dw_b[i], conv_ln_g[i], conv_ln_b[i], w_conv_proj[i],
                         w_out[i], ln1_g[i], ln1_b[i], ffn_w_up[i], ffn_dw[i], ffn_dw_b[i],
                         ffn_w_down[i], ln2_g[i], ln2_b[i])
    return h[:, N_META:]
```

```python
import contextlib, bisect, math
from collections import namedtuple
import numpy as np
import concourse.bass as bass
import concourse.mybir as mybir
from concourse.bass_utils import run_bass_kernel_spmd

F32, BF16 = mybir.dt.float32, mybir.dt.bfloat16
AF = mybir.ActivationFunctionType
ALU = mybir.AluOpType
Ref = namedtuple("Ref", "ap reg")
NEG = -30000.0
LNVAR = 2
LN_EPS = 1e-5


class Cfg:
    def __init__(s, D=1024, SEQ=2048, DFF=2816, NU=4, L=2, NSEQ=2):
        s.D, s.SEQ, s.DFF, s.NU, s.L, s.NSEQ = D, SEQ, DFF, NU, L, NSEQ
        s.KD = D // 128
        s.CW = D // 2
        s.KC = s.CW // 128
        s.KF = DFF // 128
        s.AW, s.KVW = 512, 128
        s.IN_COLS = s.AW + 2 * s.KVW + 2 * s.CW + 2 * D
        s.Sh = SEQ // NU
        s.TM = s.Sh + 16
        s.NB = s.Sh // 128
        s.alpha = (2.0 * L) ** 0.25
        o = 0
        def take(n):
            nonlocal o
            r = o; o += n; return r
        s.c_bq = take(4); s.c_bk = take(4); s.c_bcv = take(s.KC); s.c_bcg = take(s.KC)
        s.c_bga = take(s.KD); s.c_bgc = take(s.KD)
        s.c_cdw = take(31 * s.KC); s.c_cdb = take(s.KC); s.c_clg = take(s.KC); s.c_clb = take(s.KC)
        s.c_l1g = take(s.KD); s.c_l1b = take(s.KD)
        s.c_fdw = take(3 * 2 * s.KF); s.c_fdb = take(2 * s.KF)
        s.c_l2g = take(s.KD); s.c_l2b = take(s.KD)
        s.c_bq8 = take(4)
        s.NV = o


def split_tiles(n, maxw=512):
    nt = -(-n // maxw)
    w = -(-n // nt)
    w = -(-w // 16) * 16
    res, t = [], 0
    while t < n:
        ww = min(w, n - t)
        res.append((t, ww))
        t += ww
    return res


class TL:
    def __init__(s, sem, step):
        s.sem, s.step, s.val = sem, step, 0


class Buf:
    def __init__(s, P, name, width, dt, space="sbuf"):
        s.name, s.width, s.dt = name, width, dt
        if space == "sbuf":
            s.h = P.st.enter_context(P.nc.sbuf_tensor("t_" + name, [128, width], dt))
        else:
            s.h = P.st.enter_context(P.nc.psum_tensor("t_" + name, [128, width], dt))
        P.mem[name] = [[0, width, None, {}]]
        P.starts[name] = [0]

    def r(s, a, b, p0=0, p1=128):
        assert 0 <= a < b <= s.width, (s.name, a, b, s.width)
        return Ref(s.h[p0:p1, a:b], (s.name, a, b))

    def r3(s, base, n_outer, stride, inner, p0=0, p1=128):
        end = base + (n_outer - 1) * stride + inner
        assert end <= s.width, (s.name, base, n_outer, stride, inner)
        if n_outer == 1:
            return s.r(base, base + inner, p0, p1)
        rb = min(base, s.width - n_outer * stride)
        assert rb >= 0, (s.name, base, n_outer, stride, inner, s.width)
        off = base - rb
        assert off + inner <= stride, (s.name, base, n_outer, stride, inner, s.width)
        ap = s.h[p0:p1, rb:rb + n_outer * stride].rearrange("p (o s) -> p o s", s=stride)[:, :, off:off + inner]
        return Ref(ap, (s.name, base, end))


class Prog:
    ENGS = ("pe", "act", "dve", "pool", "sp")

    def __init__(s, nc, st):
        s.nc, s.st = nc, st
        s.ops = {e: [] for e in s.ENGS}
        s.known = {e: {} for e in s.ENGS}
        s.mem, s.starts = {}, {}
        s.tl = {}
        for e in ("pe", "act", "dve", "pool"):
            s.tl[e] = TL(st.enter_context(nc.semaphore("s_" + e)), 1)
        s.ndma = 0
        s.cnt = 0
        s.hook = None
        s.pending_w = set()
        s.last_use = {}

    def dma_tl(s, name):
        return TL(s.st.enter_context(s.nc.semaphore("d_" + name)), 16)

    def _touch(s, reg, stamp, is_write, needs):
        name, lo, hi = reg
        segs, starts = s.mem[name], s.starts[name]
        i = bisect.bisect_right(starts, lo) - 1
        j = i
        new = []
        wrote = False
        while j < len(segs) and segs[j][0] < hi:
            a, b, w, r = segs[j]
            if w is not None and w != stamp:
                needs.append(w)
            if is_write:
                for t, v in r.items():
                    if (t, v) != stamp:
                        needs.append((t, v))
            if a < lo:
                new.append([a, lo, w, dict(r)])
            if is_write:
                if not wrote:
                    new.append([lo, hi, stamp, {}])
                    wrote = True
            else:
                r2 = dict(r)
                r2[stamp[0]] = max(r2.get(stamp[0], 0), stamp[1])
                new.append([max(a, lo), min(b, hi), w, r2])
            if b > hi:
                new.append([hi, b, w, dict(r)])
            j += 1
        segs[i:j] = new
        starts[i:j] = [x[0] for x in new]

    def _emit(s, eng, fns, reads, writes, tl, inc, is_wload=False):
        if not is_wload:
            if s.hook is not None:
                s.hook(s.cnt)
            for k in s.pending_w:
                s.last_use[k] = s.cnt
            s.pending_w.clear()
            s.cnt += 1
        stamp = (tl, tl.val + inc)
        needs = []
        for reg in reads:
            if reg is not None:
                s._touch(reg, stamp, False, needs)
        for reg in writes:
            if reg is not None:
                s._touch(reg, stamp, True, needs)
        tl.val += inc
        kn = s.known[eng]
        waits = {}
        for t, v in needs:
            if kn.get(t, 0) < v:
                waits[t] = max(waits.get(t, 0), v)
        for t, v in waits.items():
            kn[t] = v
        s.ops[eng].append(([(t.sem, v) for t, v in waits.items()], fns, (tl.sem, inc)))

    def op(s, eng, fn, reads, writes):
        s._emit(eng, [fn], [r.reg if isinstance(r, Ref) else r for r in reads],
                [w.reg if isinstance(w, Ref) else w for w in writes], s.tl[eng], 1)

    def pe_group(s, fns, reads, writes):
        s._emit("pe", fns, [r.reg for r in reads], [w.reg for w in writes], s.tl["pe"], 1)

    def dma(s, eng, fn, reads, writes, tl, is_wload=False):
        s._emit(eng, [fn], [r.reg if isinstance(r, Ref) else r for r in reads],
                [w.reg if isinstance(w, Ref) else w for w in writes], tl, 16, is_wload)

    def op_w(s, eng, fn, reads, writes):
        s._emit(eng, [fn], [r.reg if isinstance(r, Ref) else r for r in reads],
                [w.reg if isinstance(w, Ref) else w for w in writes], s.tl[eng], 1, True)

    def wait(s, eng, tl):
        if s.known[eng].get(tl, 0) < tl.val:
            s.known[eng][tl] = tl.val
            s.ops[eng].append(([(tl.sem, tl.val)], [], None))

    def finish(s, final_tls):
        nc = s.nc
        engmap = {"pe": "tensor", "act": "scalar", "dve": "vector", "pool": "gpsimd", "sp": "sync"}
        with nc.Block() as block:
            for e in s.ENGS:
                ops = s.ops[e]

                def body(eng, ops=ops, e=e):
                    for waits, fns, sig in ops:
                        for sem, v in waits:
                            eng.wait_ge(sem, v)
                        ins = None
                        for fn in fns:
                            ins = fn(eng)
                        if ins is not None:
                            ins.then_inc(sig[0], sig[1])
                    if e == "sp":
                        for t in final_tls:
                            eng.wait_ge(t.sem, t.val)
                getattr(block, engmap[e])(body)


def build(cfg, dbg=None, plan=None):
    C = cfg
    D, KD, KC, KF, TM, L = C.D, C.KD, C.KC, C.KF, C.TM, C.L
    nc = bass.Bass("TRN2", target_bir_lowering=False)
    dt_in = lambda name, shape: nc.dram_tensor(name, list(shape), F32, kind="ExternalInput").ap()
    x_d = dt_in("x", [C.NSEQ, C.SEQ, D])
    meta_d = dt_in("meta", [16, D])
    w_in_d = dt_in("w_in", [L, D, C.IN_COLS])
    w_ap_d = dt_in("w_ap", [L, C.AW, D])
    w_cp_d = dt_in("w_cp", [L, C.CW, D])
    w_out_d = dt_in("w_out", [L, D, D])
    w_up_d = dt_in("w_up", [L, D, 2 * C.DFF])
    w_dn_d = dt_in("w_dn", [L, C.DFF, D])
    pvec_d = dt_in("pvec", [L, 128, C.NV])
    pv0_d = dt_in("pv0", [128, 2 * KD])
    bv_d = dt_in("bv", [L, 128, 128])
    sink_d = dt_in("sinkl", [L, 128, 512])
    bg_d = dt_in("bias_g", [128, 5 * 1024])
    mk_d = dt_in("mask_c", [128, 5 * 1024])
    id_d = dt_in("ident", [128, 128])
    out_d = nc.dram_tensor("out", [C.NSEQ, C.SEQ, D], F32, kind="ExternalOutput").ap()

    st = contextlib.ExitStack()
    with st:
        P = Prog(nc, st)
        B = lambda name, w, dt=BF16: Buf(P, name, w, dt)
        ps = Buf(P, "ps", 4096, F32, space="psum")
        h = B("h", KD * TM, F32)
        hb = B("hb", KD * TM)
        o_q = 0
        o_kd = o_q + 4 * TM
        o_c = o_kd + 4 * TM
        CS = 32 + TM
        o_vt = o_c + KC * CS
        o_cs = o_vt + (C.NB + 1) * 384
        o_a = o_cs + KC * TM
        o_end = o_a + 4 * TM
        assert KD * TM <= o_cs
        AW_ = max(o_end, KF * TM)
        ar = B("ar", AW_)
        ycv = B("ycv", KC * TM, F32)
        stgb = ycv if KC * TM >= 2048 else B("stg", 2048, F32)
        SW = 2048
        NSLOT = 14
        wb = B("wb", NSLOT * SW)
        slot_tl = [P.dma_tl("w%d" % i) for i in range(NSLOT)]
        yb = B("yb", KD * TM)
        ysq = B("ysq", KD * TM)
        mean_sb = B("mean_sb", 512, F32)
        m2 = B("m2", 512, F32)
        rstd = B("rstd", 512, F32)
        tmpA = B("tmpA", 512, F32)
        tmpB = B("tmpB", 512, F32)
        ub = B("ub", 2 * (16 + TM))
        gb = B("gb", 2 * (16 + TM))
        glb = B("glb", 2 * 512, F32)
        pt = B("pt", 2 * 3 * 512)
        dsb = B("dsb", 2 * 256, F32)
        iob = B("iob", 2 * D, F32)
        io_i = [0]
        pvec = B("pvec", L * C.NV, F32)
        pv0 = B("pv0", 2 * KD, F32)
        bvb = B("bvb", L * 128, F32)
        esink = B("esink", L * 512, F32)
        bm = B("bm", 5 * 1024)
        ident = B("ident", 128, F32)
        identb = B("identb", 128)
        onesz = B("onesz", 192)
        onesD = B("onesD", 128)
        onesC = B("onesC", 128)
        dg = B("dg", 64 * 128)
        dgset = [0]
        kcar = B("kcar", L * 4 * 144)
        vcar = B("vcar", L * 2 * 384)
        ccar = B("ccar", L * KC * 32)
        ucar = B("ucar", L * 2 * KF * 2)

        t_c = P.dma_tl("const")
        t_xs = [P.dma_tl("x0"), P.dma_tl("x1")]
        t_os = [P.dma_tl("o0"), P.dma_tl("o1")]

        bank_i = [0]

        def bank():
            b = bank_i[0]
            bank_i[0] = (b + 1) % 8
            return b * 512

        dram2d = {"w_in": w_in_d, "w_ap": w_ap_d, "w_cp": w_cp_d, "w_out": w_out_d, "w_up": w_up_d, "w_dn": w_dn_d}
        wl = {"req": 0, "issued": 0, "plan": [] if plan is None else plan["descs"]}
        prev_last = None if plan is None else plan["last_use"]

        def w_issue(i):
            d = wl["plan"][i]
            sl = i % NSLOT
            base = sl * SW
            if d[0] == "std":
                _, nm, l_, r0, kc, c0, ncols = d
                dst = wb.h[:, base:base + kc * ncols].rearrange("p (k c) -> p k c", k=kc)
                src = dram2d[nm][l_][r0:r0 + kc * 128, c0:c0 + ncols].rearrange("(k p) c -> p k c", p=128)
                P.dma("pool", lambda e: e.dma_start(out=dst, in_=src), [], [("wb", base, base + kc * ncols)], slot_tl[sl], True)
            else:
                _, l_, half = d
                wz = wb.r(base, base + KD * 256)
                P.op_w("pool", lambda e, wz=wz: e.memset(wz.ap, 0.0), [], [wz])
                hk = half
                for hf in range(2):
                    dst = wb.h[:, base:base + KD * 256].rearrange("p (k c) -> p k c", k=KD)[:, :, hf * 128 + hf * 64: hf * 128 + hf * 64 + 64]
                    src = w_in_d[l_][:, 512 + hk * 64: 512 + hk * 64 + 64].rearrange("(k p) c -> p k c", p=128)
                    P.dma("pool", lambda e, dst=dst, src=src: e.dma_start(out=dst, in_=src), [], [wz], slot_tl[sl], True)
            wl["issued"] = i + 1

        def w_pump(t):
            if plan is None:
                return
            while wl["issued"] < len(wl["plan"]):
                i = wl["issued"]
                if i >= NSLOT and not (prev_last.get(i - NSLOT, -1) < t):
                    break
                w_issue(i)

        P.hook = w_pump

        def w_request(desc):
            k = wl["req"]
            wl["req"] = k + 1
            if plan is None:
                wl["plan"].append(desc)
            else:
                assert wl["plan"][k] == desc, (k, wl["plan"][k], desc)
            while wl["issued"] <= k:
                w_issue(wl["issued"])
            base = (k % NSLOT) * SW
            ncols = desc[6] if desc[0] == "std" else 256

            def acc(kk, ca, cb, k=k, base=base, ncols=ncols):
                P.pending_w.add(k)
                return wb.r(base + kk * ncols + ca, base + kk * ncols + cb)
            return acc

        def load_w(nm, l_, r0, kc, c0, ncols):
            if kc * ncols <= SW:
                return w_request(("std", nm, l_, r0, kc, c0, ncols))
            assert ncols % 256 == 0 and kc * 256 <= SW
            subs = [w_request(("std", nm, l_, r0, kc, c0 + cc, 256)) for cc in range(0, ncols, 256)]

            def acc(kk, ca, cb):
                si = ca // 256
                assert (cb - 1) // 256 == si
                return subs[si](kk, ca - si * 256, cb - si * 256)
            return acc

        def load_wk(l_):
            subs = [w_request(("k", l_, half)) for half in range(2)]

            def acc(kk, ca, cb):
                si = ca // 256
                assert (cb - 1) // 256 == si
                return subs[si](kk, ca - si * 256, cb - si * 256)
            return acc

        def _f_A(o, l, r, a, b): return lambda e: e.matmul(o.ap, l.ap, r.ap, start=a, stop=b)
        def _f_B(o, l, r, a, b): return lambda e: e.matmul(o.ap, l.ap, r.ap, start=a, stop=b)
        def _f_L(o, l, r, a, b): return lambda e: e.matmul(o.ap, l.ap, r.ap, start=a, stop=b)
        def _f_C(o, l, r, a, b): return lambda e: e.matmul(o.ap, l.ap, r.ap, start=a, stop=b)
        def _f_D(o, l, r, a, b): return lambda e: e.matmul(o.ap, l.ap, r.ap, start=a, stop=b)
        def _f_E(o, l, r, a, b): return lambda e: e.matmul(o.ap, l.ap, r.ap, start=a, stop=b)
        def _f_U(o, l, r, a, b): return lambda e: e.matmul(o.ap, l.ap, r.ap, start=a, stop=b)
        def _f_V(o, l, r, a, b): return lambda e: e.matmul(o.ap, l.ap, r.ap, start=a, stop=b)
        def _f_W(o, l, r, a, b): return lambda e: e.matmul(o.ap, l.ap, r.ap, start=a, stop=b)
        facs = {"A": _f_A, "B": _f_B, "L": _f_L, "C": _f_C, "D": _f_D, "E": _f_E, "U": _f_U, "V": _f_V, "W": _f_W}

        def mm(out, pairs, tag="A", first=True, last=True):
            n = len(pairs)
            fac = facs[tag]
            fns = [fac(out, l, r, first and i == 0, last and i == n - 1) for i, (l, r) in enumerate(pairs)]
            P.pe_group(fns, [x for p in pairs for x in p], [out])

        def pcol(l, c):
            return pvec.r(l * C.NV + c, l * C.NV + c + 1)

        def cdma(dst, src):
            P.dma("sp", lambda e: e.dma_start(out=dst.ap, in_=src), [], [dst], t_c)
            P.wait("sp", t_c)

        cdma(ident.r(0, 128), id_d)
        cdma(pv0.r(0, 2 * KD), pv0_d)
        for l in range(L):
            cdma(pvec.r(l * C.NV, (l + 1) * C.NV), pvec_d[l])
            cdma(bvb.r(l * 128, (l + 1) * 128), bv_d[l])
            cdma(esink.r(l * 512, (l + 1) * 512), sink_d[l])
        P.op("dve", lambda e: e.tensor_copy(identb.r(0, 128).ap, ident.r(0, 128).ap), [ident.r(0, 128)], [identb.r(0, 128)])
        P.op("dve", lambda e: e.memset(onesz.r(0, 192).ap, 0.0), [], [onesz.r(0, 192)])
        P.op("dve", lambda e: e.memset(onesz.r(64, 128).ap, 1.0), [], [onesz.r(64, 128)])
        P.op("dve", lambda e: e.memset(onesD.r(0, 128).ap, 1.0 / D), [], [onesD.r(0, 128)])
        P.op("dve", lambda e: e.memset(onesC.r(0, 128).ap, 1.0 / C.CW), [], [onesC.r(0, 128)])
        P.op("dve", lambda e: e.memset(ar.r(0, AW_).ap, 0.0), [], [ar.r(0, AW_)])
        P.op("dve", lambda e: e.memset(vcar.r(0, L * 768).ap, 0.0), [], [vcar.r(0, L * 768)])
        P.op("dve", lambda e: e.memset(ub.r(0, 2 * (16 + TM)).ap, 0.0), [], [ub.r(0, 2 * (16 + TM))])
        P.op("dve", lambda e: e.memset(gb.r(0, 2 * (16 + TM)).ap, 0.0), [], [gb.r(0, 2 * (16 + TM))])
        for i in range(5):
            s1, s2 = stgb.r(0, 1024), stgb.r(1024, 2048)
            cdma(s1, bg_d[:, i * 1024:(i + 1) * 1024])
            cdma(s2, mk_d[:, i * 1024:(i + 1) * 1024])
            o = bm.r(i * 1024, (i + 1) * 1024)
            P.op("dve", lambda e, o=o, s1=s1, s2=s2: e.tensor_tensor(o.ap, s1.ap, s2.ap, ALU.add), [s1, s2], [o])
        for l in range(L):
            o = esink.r(l * 512, (l + 1) * 512)
            P.op("act", lambda e, o=o: e.activation(o.ap, o.ap, AF.Exp), [o], [o])
            o8 = pvec.r(l * C.NV + C.c_bq8, l * C.NV + C.c_bq8 + 4)
            i8 = pvec.r(l * C.NV + C.c_bq, l * C.NV + C.c_bq + 4)
            P.op("dve", lambda e, o8=o8, i8=i8: e.tensor_scalar(o8.ap, i8.ap, 0.125, None, ALU.mult), [i8], [o8])

        def ln_pre(ybuf, ystride, c, t0, n):
            yc = ybuf.r(c * ystride + t0, c * ystride + t0 + n)
            ybc = yb.r(c * TM + t0, c * TM + t0 + n)
            ysc = ysq.r(c * TM + t0, c * TM + t0 + n)
            P.op("act", lambda e: e.activation(ybc.ap, yc.ap, AF.Identity), [yc], [ybc])
            P.op("act", lambda e: e.activation(ysc.ap, yc.ap, AF.Square), [yc], [ysc])

        def ln_stats_part1(nch, ones_ref, tiles_):
            res = {}
            assert 3 * len(tiles_) <= 7
            for (t0, n) in tiles_:
                pm, pq = bank(), bank()
                pmr, pqr = ps.r(pm, pm + n), ps.r(pq, pq + n)
                mm(pmr, [(ones_ref, yb.r(c * TM + t0, c * TM + t0 + n)) for c in range(nch - 1)], "L", True, False)
                mm(pqr, [(ones_ref, ysq.r(c * TM + t0, c * TM + t0 + n)) for c in range(nch - 1)], "L", True, False)
                res[t0] = (pm, pq)
            return res

        def layer_norm(ybuf, ystride, nch, ones_ref, t0, n, gcol, bcol, outs, part1=None, defer=None):
            if part1 is None:
                pm = bank()
                pq = bank()
                pmr, pqr = ps.r(pm, pm + n), ps.r(pq, pq + n)
                mm(pmr, [(ones_ref, yb.r(c * TM + t0, c * TM + t0 + n)) for c in range(nch)], "L")
                mm(pqr, [(ones_ref, ysq.r(c * TM + t0, c * TM + t0 + n)) for c in range(nch)], "L")
            else:
                pm, pq = part1[t0]
                pmr, pqr = ps.r(pm, pm + n), ps.r(pq, pq + n)
                c = nch - 1
                mm(pmr, [(ones_ref, yb.r(c * TM + t0, c * TM + t0 + n))], "L", False, True)
                mm(pqr, [(ones_ref, ysq.r(c * TM + t0, c * TM + t0 + n))], "L", False, True)
            ms, m2r, rs = mean_sb.r(0, n), m2.r(0, n), rstd.r(0, n)
            P.op("dve", lambda e: e.tensor_copy(ms.ap, pmr.ap), [pmr], [ms])
            if LNVAR & 1:
                P.op("act", lambda e: e.activation(m2r.ap, pmr.ap, AF.Square), [pmr], [m2r])
            else:
                P.op("dve", lambda e: e.tensor_tensor(m2r.ap, ms.ap, ms.ap, ALU.mult), [ms], [m2r])
            if LNVAR & 2:
                P.op("dve", lambda e: e.scalar_tensor_tensor(m2r.ap, pqr.ap, LN_EPS, m2r.ap, ALU.add, ALU.subtract), [pqr, m2r], [m2r])
            else:
                P.op("dve", lambda e: e.tensor_tensor(m2r.ap, pqr.ap, m2r.ap, ALU.subtract), [pqr, m2r], [m2r])
                P.op("dve", lambda e: e.tensor_scalar(m2r.ap, m2r.ap, LN_EPS, None, ALU.add), [m2r], [m2r])
            P.op("act", lambda e: e.activation(rs.ap, m2r.ap, AF.Ln), [m2r], [rs])
            P.op("act", lambda e: e.activation(rs.ap, rs.ap, AF.Exp, scale=-0.5), [rs], [rs])
            for c in range(nch):
                yc = ybuf.r(c * ystride + t0, c * ystride + t0 + n)
                P.op("dve", lambda e, yc=yc: e.tensor_tensor(yc.ap, yc.ap, ms.ap, ALU.subtract), [yc, ms], [yc])
            if defer is not None:
                for c in range(nch):
                    yc = ybuf.r(c * ystride + t0, c * ystride + t0 + n)
                    P.op("dve", lambda e, yc=yc: e.tensor_tensor(yc.ap, yc.ap, rs.ap, ALU.mult), [yc, rs], [yc])

                def emit_outs():
                    for c in range(nch):
                        yc = ybuf.r(c * ystride + t0, c * ystride + t0 + n)
                        for (buf, stride, func) in outs:
                            oc = buf.r(c * stride + t0, c * stride + t0 + n)
                            bc_, gc_ = bcol(c), gcol(c)
                            P.op("act", lambda e, oc=oc, yc=yc, func=func, bc_=bc_, gc_=gc_: e.activation(
                                oc.ap, yc.ap, func, bias=bc_.ap, scale=gc_.ap), [yc, gc_, bc_], [oc])
                defer.append(emit_outs)
                return
            for c in range(nch):
                yc = ybuf.r(c * ystride + t0, c * ystride + t0 + n)
                P.op("dve", lambda e, yc=yc: e.tensor_tensor(yc.ap, yc.ap, rs.ap, ALU.mult), [yc, rs], [yc])
                for (buf, stride, func) in outs[:1]:
                    oc = buf.r(c * stride + t0, c * stride + t0 + n)
                    bc_, gc_ = bcol(c), gcol(c)
                    P.op("act", lambda e, oc=oc, yc=yc, func=func, bc_=bc_, gc_=gc_: e.activation(
                        oc.ap, yc.ap, func, bias=bc_.ap, scale=gc_.ap), [yc, gc_, bc_], [oc])
            for c in range(nch):
                yc = ybuf.r(c * ystride + t0, c * ystride + t0 + n)
                for (buf, stride, func) in outs[1:]:
                    oc = buf.r(c * stride + t0, c * stride + t0 + n)
                    bc_, gc_ = bcol(c), gcol(c)
                    P.op("act", lambda e, oc=oc, yc=yc, func=func, bc_=bc_, gc_=gc_: e.activation(
                        oc.ap, yc.ap, func, bias=bc_.ap, scale=gc_.ap), [yc, gc_, bc_], [oc])

        def h_ln(gcol, bcol, t0, n, part1=None, need_hb=True):
            outs = [(hb, TM, AF.Identity), (h, TM, AF.Identity)] if need_hb else [(h, TM, AF.Identity)]
            layer_norm(h, TM, KD, onesD.r(0, 128), t0, n, gcol, bcol, outs, part1)

        for sq in range(C.NSEQ):
            for u in range(C.NU):
                off0 = 16 if u == 0 else 0
                Tu = C.Sh + off0
                tiles = split_tiles(Tu)
                blocks = []
                if u == 0:
                    blocks.append((0, 16, "meta"))
                for i in range(C.NB):
                    blocks.append((off0 + 128 * i, 128, u * C.Sh + 128 * i))
                for (t0, n, src) in blocks:
                    io_i[0] ^= 1
                    xo = io_i[0] * D
                    xr = iob.r(xo, xo + D, 0, n)
                    srcap = meta_d if src == "meta" else x_d[sq, src:src + n, :]
                    P.dma("sp", lambda e, xr=xr, srcap=srcap: e.dma_start(out=xr.ap, in_=srcap), [], [iob.r(xo, xo + D)], t_xs[io_i[0]])
                    for c0 in range(0, KD, 4):
                        bk = bank()
                        for c in range(c0, min(c0 + 4, KD)):
                            o = ps.r(bk + (c - c0) * 128, bk + (c - c0) * 128 + n)
                            i_ = iob.r(xo + c * 128, xo + (c + 1) * 128, 0, n)
                            idr = ident.r(0, n, 0, n)
                            P.pe_group([lambda e, o=o, i_=i_, idr=idr: e.transpose(o.ap, i_.ap, idr.ap)], [i_, idr], [o])
                        nch = min(4, KD - c0)
                        src3 = ps.r3(bk, nch, 128, n)
                        dst3 = h.r3(c0 * TM + t0, nch, TM, n)
                        P.op("dve", lambda e, src3=src3, dst3=dst3: e.tensor_copy(dst3.ap, src3.ap), [src3], [dst3])
                        for c in range(c0, c0 + nch):
                            ln_pre(h, TM, c, t0, n)
                for (t0, n) in tiles:
                    h_ln(lambda c: pv0.r(c, c + 1), lambda c: pv0.r(KD + c, KD + c + 1), t0, n)

                for l in range(L if dbg is None else dbg):
                    NVl = l * C.NV
                    hst = [None]
                    dpar = [0]
                    hbr = lambda k, t0, n: hb.r(k * TM + t0, k * TM + t0 + n)
                    def conv_diags(j):
                        dbase = (j % 2) * 32 * 128
                        for tap in range(31):
                            dgr = dg.r(dbase + tap * 128, dbase + tap * 128 + 128)
                            wc_ = pcol(l, C.c_cdw + tap * KC + j)
                            P.op("dve", lambda e, dgr=dgr, wc_=wc_: e.tensor_scalar(dgr.ap, identb.r(0, 128).ap, wc_.ap, None, ALU.mult),
                                 [identb.r(0, 128), wc_], [dgr])

                    conv_diags(0)
                    wq = load_w("w_in", l, 0, KD, 0, 512)
                    for j in range(4):
                        for (t0, n) in tiles:
                            bk = bank()
                            pr = ps.r(bk, bk + n)
                            mm(pr, [(wq(k, j * 128, (j + 1) * 128), hbr(k, t0, n)) for k in range(KD)])
                            o = ar.r(o_q + j * TM + t0, o_q + j * TM + t0 + n)
                            bc = pcol(l, C.c_bq8 + j)
                            P.op("act", lambda e, o=o, pr=pr, bc=bc: e.activation(o.ap, pr.ap, AF.Identity, bias=bc.ap, scale=0.125), [pr, bc], [o])
                    wk = load_wk(l)
                    for v in range(4):
                        for (t0, n) in tiles:
                            bk = bank()
                            pr = ps.r(bk, bk + n)
                            mm(pr, [(wk(k, v * 128, (v + 1) * 128), hbr(k, t0, n)) for k in range(KD)])
                            o = ar.r(o_kd + v * TM + t0, o_kd + v * TM + t0 + n)
                            bc = pcol(l, C.c_bk + v)
                            P.op("act", lambda e, o=o, pr=pr, bc=bc: e.activation(o.ap, pr.ap, AF.Identity, bias=bc.ap, scale=1.0), [pr, bc], [o])
                    wv = load_w("w_in", l, 0, KD, 640, 128)
                    for bi, (t0, n, _) in enumerate(blocks):
                        bk = bank()
                        pr = ps.r(bk, bk + 128, 0, n)
                        mm(pr, [(hbr(k, t0, n), wv(k, 0, 128)) for k in range(KD)])
                        o = ar.r3(o_vt + bi * 384 + 64, 2, 192, 64, 0, n)
                        pr3 = ps.r3(bk, 2, 64, 64, 0, n)
                        bvr = bvb.r3(l * 128, 2, 64, 64, 0, n)
                        P.op("dve", lambda e, o=o, pr3=pr3, bvr=bvr: e.tensor_tensor(o.ap, pr3.ap, bvr.ap, ALU.add), [pr3, bvr], [o])
                    for j0 in range(0, KC, 4):
                        nj = min(4, KC - j0)
                        wcv = load_w("w_in", l, 0, KD, 768 + j0 * 128, nj * 128)
                        wcg = load_w("w_in", l, 0, KD, 768 + C.CW + j0 * 128, nj * 128)
                        for jj in range(nj):
                            j = j0 + jj
                            for (t0, n) in tiles:
                                b1, b2 = bank(), bank()
                                p1, p2 = ps.r(b1, b1 + n), ps.r(b2, b2 + n)
                                mm(p1, [(wcv(k, jj * 128, (jj + 1) * 128), hbr(k, t0, n)) for k in range(KD)])
                                mm(p2, [(wcg(k, jj * 128, (jj + 1) * 128), hbr(k, t0, n)) for k in range(KD)])
                                sg = tmpA.r(0, n)
                                bcg, bcv = pcol(l, C.c_bcg + j), pcol(l, C.c_bcv + j)
                                P.op("act", lambda e, sg=sg, p2=p2, bcg=bcg: e.activation(sg.ap, p2.ap, AF.Sigmoid, bias=bcg.ap, scale=1.0), [p2, bcg], [sg])
                                o = ar.r(o_c + j * CS + 32 + t0, o_c + j * CS + 32 + t0 + n)
                                P.op("dve", lambda e, o=o, p1=p1, bcv=bcv, sg=sg: e.scalar_tensor_tensor(
                                    o.ap, p1.ap, bcv.ap, sg.ap, ALU.add, ALU.mult), [p1, bcv, sg], [o])
                    kc_base = l * 4 * 144
                    vc_base = l * 768
                    cc_base = l * KC * 32
                    if u == 0:
                        for v in range(4):
                            s_ = ar.r(o_kd + v * TM, o_kd + v * TM + 16)
                            d_ = kcar.r(kc_base + v * 144, kc_base + v * 144 + 16)
                            P.op("pool", lambda e, s_=s_, d_=d_: e.tensor_copy(d_.ap, s_.ap), [s_], [d_])
                        s_ = ar.r(o_vt, o_vt + 384, 0, 16)
                        d_ = vcar.r(vc_base, vc_base + 384, 0, 16)
                        P.op("pool", lambda e, s_=s_, d_=d_: e.tensor_copy(d_.ap, s_.ap), [s_], [d_])
                        for j in range(KC):
                            z = ar.r(o_c + j * CS, o_c + j * CS + 32)
                            P.op("pool", lambda e, z=z: e.memset(z.ap, 0.0), [], [z])
                    else:
                        for j in range(KC):
                            z = ar.r(o_c + j * CS, o_c + j * CS + 32)
                            s_ = ccar.r(cc_base + j * 32, cc_base + j * 32 + 32)
                            P.op("pool", lambda e, z=z, s_=s_: e.tensor_copy(z.ap, s_.ap), [s_], [z])

                    for j in range(KC):
                        bks = [bank() for _ in tiles]
                        if j + 1 < KC:
                            conv_diags(j + 1)
                        dbase = (j % 2) * 32 * 128
                        for ti, (t0, n) in enumerate(tiles):
                            pr = ps.r(bks[ti], bks[ti] + n)
                            prs = []
                            for tap in range(31):
                                dgr = dg.r(dbase + tap * 128, dbase + tap * 128 + 128)
                                rb = o_c + j * CS + 2 + tap + t0
                                prs.append((dgr, ar.r(rb, rb + n)))
                            mm(pr, prs, "B")
                        for ti, (t0, n) in enumerate(tiles):
                            pr = ps.r(bks[ti], bks[ti] + n)
                            o = ycv.r(j * TM + t0, j * TM + t0 + n)
                            bc = pcol(l, C.c_cdb + j)
                            P.op("act", lambda e, o=o, pr=pr, bc=bc: e.activation(o.ap, pr.ap, AF.Identity, bias=bc.ap, scale=1.0), [pr, bc], [o])
                            ln_pre(ycv, TM, j, t0, n)
                    if u + 1 < C.NU:
                        for j in range(KC):
                            s_ = ar.r(o_c + j * CS + Tu, o_c + j * CS + Tu + 32)
                            d_ = ccar.r(cc_base + j * 32, cc_base + j * 32 + 32)
                            P.op("pool", lambda e, s_=s_, d_=d_: e.tensor_copy(d_.ap, s_.ap), [s_], [d_])
                    cl_defer = []

                    def conv_ln():
                        for (t0, n) in tiles:
                            layer_norm(ycv, TM, KC, onesC.r(0, 128), t0, n,
                                       lambda c: pcol(l, C.c_clg + c), lambda c: pcol(l, C.c_clb + c),
                                       [(_CSBuf(ar, o_cs), TM, AF.Silu)], defer=cl_defer)

                    att_items = []
                    for bi, (t0, nq, src) in enumerate(blocks):
                        kts = []
                        if src == "meta":
                            kts.append(("m0", 16, lambda v: ar.r(o_kd + v * TM, o_kd + v * TM + 16),
                                        lambda hk, par: ar.r(o_vt + hk * 192 + 64 - 64 * par, o_vt + hk * 192 + 192 - 64 * par, 0, 16), 4))
                        else:
                            first_of_seq = (u == 0 and bi == 1)
                            kts.append(("cur", 128, lambda v, t0=t0: ar.r(o_kd + v * TM + t0, o_kd + v * TM + t0 + 128),
                                        lambda hk, par, bi=bi: ar.r(o_vt + bi * 384 + hk * 192 + 64 - 64 * par, o_vt + bi * 384 + hk * 192 + 192 - 64 * par), 0))
                            if not first_of_seq:
                                if bi == 0:
                                    kts.append(("prev", 128, lambda v: kcar.r(kc_base + v * 144 + 16, kc_base + v * 144 + 144),
                                                lambda hk, par: vcar.r(vc_base + 384 + hk * 192 + 64 - 64 * par, vc_base + 384 + hk * 192 + 192 - 64 * par), 1))
                                else:
                                    kts.append(("prev", 128, lambda v, t0=t0: ar.r(o_kd + v * TM + t0 - 128, o_kd + v * TM + t0),
                                                lambda hk, par, bi=bi: ar.r(o_vt + (bi - 1) * 384 + hk * 192 + 64 - 64 * par, o_vt + (bi - 1) * 384 + hk * 192 + 192 - 64 * par), 1))
                            kts.append(("meta", 16, lambda v: kcar.r(kc_base + v * 144, kc_base + v * 144 + 16),
                                        lambda hk, par: vcar.r(vc_base + hk * 192 + 64 - 64 * par, vc_base + hk * 192 + 192 - 64 * par, 0, 16),
                                        2 if first_of_seq else 3))
                        for hk in range(2):
                            att_items.append((t0, nq, kts, hk))

                    def att_p1(idx, t0, nq, kts, hk):
                        pbase = (idx % 2) * 1536
                        pts = []
                        for ki, (kind, nk, kref, vref, bmi) in enumerate(kts):
                            bk = bank()
                            ofull = ps.r(bk, bk + 4 * nq, 0, nk)
                            bmfull = bm.r(bmi * 1024 + hk * 512, bmi * 1024 + hk * 512 + 4 * nq, 0, nk)
                            idr = identb.r(0, nk, 0, nk)
                            fns = [lambda e, o=ofull, idr=idr, bmr=bmfull: e.matmul(o.ap, idr.ap, bmr.ap, start=True, stop=False)]
                            rds = [idr, bmfull]
                            for hf in range(2):
                                o = ps.r(bk + hf * 2 * nq, bk + (hf + 1) * 2 * nq, 0, nk)
                                kr = kref(hk * 2 + hf)
                                qr = ar.r3(o_q + (2 * hk) * TM + t0, 2, TM, nq)
                                fns.append(lambda e, o=o, kr=kr, qr=qr, hf=hf: e.matmul(o.ap, kr.ap, qr.ap, start=False, stop=(hf == 1)))
                                rds += [kr, qr]
                            P.pe_group(fns, rds, [ofull])
                            pr = ps.r(bk, bk + 4 * nq, 0, nk)
                            pp = pt.r(pbase + ki * 512, pbase + ki * 512 + 4 * nq, 0, nk)
                            P.op("act", lambda e, pp=pp, pr=pr: e.activation(pp.ap, pr.ap, AF.Exp), [pr], [pp])
                            pts.append((pbase + ki * 512, nk, vref))
                        return pts

                    def att_p2(idx, t0, nq, hk, pts):
                        bo = bank()
                        prs_o, prs_d = [], []
                        for (pb, nk, vref) in pts:
                            for par in range(2):
                                rhs = pt.r(pb + par * 2 * nq, pb + (par + 1) * 2 * nq, 0, nk)
                                prs_o.append((vref(hk, par), rhs))
                                oz = onesz.r(64 - 64 * par, 192 - 64 * par, 0, nk)
                                prs_d.append((oz, rhs))
                        O3 = ps.r3(bo, 2, nq, nq)
                        D3 = ps.r3(bo + 256, 2, nq, nq)
                        mm(O3, prs_o, "C")
                        mm(D3, prs_d, "C")
                        ds = dsb.r3((idx % 2) * 256, 2, nq, nq)
                        es = esink.r3(l * 512 + hk * 256, 2, 128, nq)
                        P.op("dve", lambda e, ds=ds, D3=D3, es=es: e.tensor_tensor(ds.ap, D3.ap, es.ap, ALU.add), [D3, es], [ds])
                        P.op("dve", lambda e, ds=ds: e.reciprocal(ds.ap, ds.ap), [ds], [ds])
                        ao = ar.r3(o_a + (2 * hk) * TM + t0, 2, TM, nq)
                        P.op("dve", lambda e, ao=ao, O3=O3, ds=ds: e.tensor_tensor(ao.ap, O3.ap, ds.ap, ALU.mult), [O3, ds], [ao])

                    prev = None
                    cl_at = max(len(att_items) - 2, 0)
                    for idx, (t0, nq, kts, hk) in enumerate(att_items):
                        pts = att_p1(idx, t0, nq, kts, hk)
                        if prev is not None:
                            att_p2(*prev)
                        prev = (idx, t0, nq, hk, pts)
                        if idx == cl_at:
                            conv_ln()
                        if idx == len(att_items) - 1:
                            for fdef in cl_defer:
                                fdef()
                    att_p2(*prev)
                    if u + 1 < C.NU:
                        lt0 = Tu - 128
                        for v in range(4):
                            s_ = ar.r(o_kd + v * TM + lt0, o_kd + v * TM + lt0 + 128)
                            d_ = kcar.r(kc_base + v * 144 + 16, kc_base + v * 144 + 144)
                            P.op("pool", lambda e, s_=s_, d_=d_: e.tensor_copy(d_.ap, s_.ap), [s_], [d_])
                        lb = len(blocks) - 1
                        s_ = ar.r(o_vt + lb * 384, o_vt + lb * 384 + 384)
                        d_ = vcar.r(vc_base + 384, vc_base + 768)
                        P.op("pool", lambda e, s_=s_, d_=d_: e.tensor_copy(d_.ap, s_.ap), [s_], [d_])

                    for c0 in range(0, D, 512):
                        ncol = min(512, D - c0)
                        wga = load_w("w_in", l, 0, KD, 768 + 2 * C.CW + c0, ncol)
                        wgc = load_w("w_in", l, 0, KD, 768 + 2 * C.CW + D + c0, ncol)
                        wap = load_w("w_ap", l, 0, 4, c0, ncol)
                        wcp = load_w("w_cp", l, 0, KC, c0, ncol)
                        for mm_ in range(ncol // 128):
                            m = c0 // 128 + mm_
                            ca, cb = mm_ * 128, (mm_ + 1) * 128
                            for (t0, n) in tiles:
                                b1, b2, b3, b4 = bank(), bank(), bank(), bank()
                                pya, pyc, pga, pgc = (ps.r(b, b + n) for b in (b1, b2, b3, b4))
                                mm(pga, [(wga(k, ca, cb), hbr(k, t0, n)) for k in range(KD)], "D")
                                mm(pgc, [(wgc(k, ca, cb), hbr(k, t0, n)) for k in range(KD)], "D")
                                mm(pya, [(wap(k, ca, cb), ar.r(o_a + k * TM + t0, o_a + k * TM + t0 + n)) for k in range(4)], "D")
                                mm(pyc, [(wcp(k, ca, cb), ar.r(o_cs + k * TM + t0, o_cs + k * TM + t0 + n)) for k in range(KC)], "D")
                                dpar[0] ^= 1
                                if dpar[0]:
                                    sa, sc = tmpA.r(0, n), tmpB.r(0, n)
                                else:
                                    sa, sc = glb.r(0, n), glb.r(512, 512 + n)
                                t1, t2 = sa, sc
                                ba_, bc_ = pcol(l, C.c_bga + m), pcol(l, C.c_bgc + m)
                                P.op("act", lambda e, sa=sa, pga=pga, ba_=ba_: e.activation(sa.ap, pga.ap, AF.Sigmoid, bias=ba_.ap, scale=1.0), [pga, ba_], [sa])
                                P.op("act", lambda e, sc=sc, pgc=pgc, bc_=bc_: e.activation(sc.ap, pgc.ap, AF.Sigmoid, bias=bc_.ap, scale=1.0), [pgc, bc_], [sc])
                                P.op("dve", lambda e, t1=t1, pya=pya, sa=sa: e.tensor_tensor(t1.ap, pya.ap, sa.ap, ALU.mult), [pya, sa], [t1])
                                P.op("dve", lambda e, t2=t2, pyc=pyc, sc=sc: e.tensor_tensor(t2.ap, pyc.ap, sc.ap, ALU.mult), [pyc, sc], [t2])
                                o = ar.r(m * TM + t0, m * TM + t0 + n)
                                P.op("dve", lambda e, o=o, t1=t1, t2=t2: e.tensor_tensor(o.ap, t1.ap, t2.ap, ALU.add), [t1, t2], [o])
                    for c0 in range(0, D, 512):
                        ncol = min(512, D - c0)
                        wo = load_w("w_out", l, 0, KD, c0, ncol)
                        for mm_ in range(ncol // 128):
                            m = c0 // 128 + mm_
                            for (t0, n) in tiles:
                                bk = bank()
                                pr = ps.r(bk, bk + n)
                                mm(pr, [(wo(k, mm_ * 128, (mm_ + 1) * 128), ar.r(k * TM + t0, k * TM + t0 + n)) for k in range(KD)], "E")
                                hr = h.r(m * TM + t0, m * TM + t0 + n)
                                P.op("dve", lambda e, hr=hr, pr=pr: e.scalar_tensor_tensor(hr.ap, hr.ap, C.alpha, pr.ap, ALU.mult, ALU.add), [hr, pr], [hr])
                                ln_pre(h, TM, m, t0, n)
                    hst[0] = ln_stats_part1(KD, onesD.r(0, 128), tiles)
                    for (t0, n) in tiles:
                        h_ln(lambda c: pcol(l, C.c_l1g + c), lambda c: pcol(l, C.c_l1b + c), t0, n, hst[0])

                    uc_base = l * 2 * KF * 2
                    UBW = 16 + TM
                    wcur = {}

                    def ffn_a(j):
                        j0, jj = (j // 4) * 4, j % 4
                        if jj == 0:
                            nj = min(4, KF - j0)
                            wcur["u"] = load_w("w_up", l, 0, KD, j0 * 128, nj * 128)
                            wcur["g"] = load_w("w_up", l, 0, KD, C.DFF + j0 * 128, nj * 128)
                        wu, wg = wcur["u"], wcur["g"]
                        ob = (j % 2) * UBW
                        for (xb, which) in ((ub, 0), (gb, 1)):
                            hz = xb.r(ob + 14, ob + 16)
                            cr = ucar.r(uc_base + (which * KF + j) * 2, uc_base + (which * KF + j) * 2 + 2)
                            if u == 0:
                                P.op("pool", lambda e, hz=hz: e.memset(hz.ap, 0.0), [], [hz])
                            else:
                                P.op("pool", lambda e, hz=hz, cr=cr: e.tensor_copy(hz.ap, cr.ap), [cr], [hz])
                        for (t0, n) in tiles:
                            b1, b2 = bank(), bank()
                            p1, p2 = ps.r(b1, b1 + n), ps.r(b2, b2 + n)
                            mm(p1, [(wu(k, jj * 128, (jj + 1) * 128), hbr(k, t0, n)) for k in range(KD)], "U")
                            mm(p2, [(wg(k, jj * 128, (jj + 1) * 128), hbr(k, t0, n)) for k in range(KD)], "U")
                            o1, o2 = ub.r(ob + 16 + t0, ob + 16 + t0 + n), gb.r(ob + 16 + t0, ob + 16 + t0 + n)
                            P.op("act", lambda e, o1=o1, p1=p1: e.activation(o1.ap, p1.ap, AF.Identity), [p1], [o1])
                            P.op("dve", lambda e, o2=o2, p2=p2: e.tensor_copy(o2.ap, p2.ap), [p2], [o2])
                        if u + 1 < C.NU:
                            for (xb, which) in ((ub, 0), (gb, 1)):
                                s_ = xb.r(ob + 16 + Tu - 2, ob + 16 + Tu)
                                cr = ucar.r(uc_base + (which * KF + j) * 2, uc_base + (which * KF + j) * 2 + 2)
                                P.op("pool", lambda e, s_=s_, cr=cr: e.tensor_copy(cr.ap, s_.ap), [s_], [cr])

                    def ffn_b0(j):
                        dbase = (j % 2) * 32 * 128
                        for which in (0, 1):
                            for tap in range(3):
                                dgr = dg.r(dbase + (which * 3 + tap) * 128, dbase + (which * 3 + tap + 1) * 128)
                                wc_ = pcol(l, C.c_fdw + tap * 2 * KF + which * KF + j)
                                P.op("dve", lambda e, dgr=dgr, wc_=wc_: e.tensor_scalar(dgr.ap, identb.r(0, 128).ap, wc_.ap, None, ALU.mult),
                                     [identb.r(0, 128), wc_], [dgr])

                    def ffn_b(j):
                        ob = (j % 2) * UBW
                        cbk = [[bank() for _ in tiles] for _ in range(2)]
                        dbase = (j % 2) * 32 * 128
                        for which, xb in ((0, ub), (1, gb)):
                            for ti, (t0, n) in enumerate(tiles):
                                pr = ps.r(cbk[which][ti], cbk[which][ti] + n)
                                prs = []
                                for tap in range(3):
                                    dgr = dg.r(dbase + (which * 3 + tap) * 128, dbase + (which * 3 + tap + 1) * 128)
                                    prs.append((dgr, xb.r(ob + 14 + tap + t0, ob + 14 + tap + t0 + n)))
                                mm(pr, prs, "V")
                        for ti, (t0, n) in enumerate(tiles):
                            pu = ps.r(cbk[0][ti], cbk[0][ti] + n)
                            pg = ps.r(cbk[1][ti], cbk[1][ti] + n)
                            gl = glb.r((j % 2) * 512, (j % 2) * 512 + n)
                            bgc_, buc_ = pcol(l, C.c_fdb + KF + j), pcol(l, C.c_fdb + j)
                            P.op("act", lambda e, gl=gl, pg=pg, bgc_=bgc_: e.activation(gl.ap, pg.ap, AF.Gelu, bias=bgc_.ap, scale=1.0), [pg, bgc_], [gl])
                            o = ar.r(j * TM + t0, j * TM + t0 + n)
                            P.op("dve", lambda e, o=o, pu=pu, buc_=buc_, gl=gl: e.scalar_tensor_tensor(
                                o.ap, pu.ap, buc_.ap, gl.ap, ALU.add, ALU.mult), [pu, buc_, gl], [o])

                    ffn_b0(0)
                    ffn_a(0)
                    for j in range(KF):
                        if j + 1 < KF:
                            ffn_b0(j + 1)
                            ffn_a(j + 1)
                        ffn_b(j)
                    kgs = [(k0, min(8, KF - k0)) for k0 in range(0, KF, 8)]
                    assert len(kgs) <= NSLOT - 1
                    for c0 in range(0, D, 512):
                        ncol = min(512, D - c0)
                        wds = [load_w("w_dn", l, k0 * 128, kn, c0, ncol) for (k0, kn) in kgs]
                        for mm_ in range(ncol // 128):
                            m = c0 // 128 + mm_
                            for (t0, n) in tiles:
                                bk = bank()
                                pr = ps.r(bk, bk + n)
                                prs = []
                                for gi, (k0, kn) in enumerate(kgs):
                                    for k in range(kn):
                                        prs.append((wds[gi](k, mm_ * 128, (mm_ + 1) * 128), ar.r((k0 + k) * TM + t0, (k0 + k) * TM + t0 + n)))
                                mm(pr, prs, "W")
                                hr = h.r(m * TM + t0, m * TM + t0 + n)
                                P.op("dve", lambda e, hr=hr, pr=pr: e.scalar_tensor_tensor(hr.ap, hr.ap, C.alpha, pr.ap, ALU.mult, ALU.add), [hr, pr], [hr])
                                ln_pre(h, TM, m, t0, n)
                    hst[0] = ln_stats_part1(KD, onesD.r(0, 128), tiles)
                    for (t0, n) in tiles:
                        h_ln(lambda c: pcol(l, C.c_l2g + c), lambda c: pcol(l, C.c_l2b + c), t0, n, hst[0], need_hb=(l + 1 < (L if dbg is None else dbg)))
                    zr = ar.r(o_vt, o_vt + (C.NB + 1) * 384)
                    P.op("pool", lambda e, zr=zr: e.memset(zr.ap, 0.0), [], [zr])

                for (t0, n, src) in blocks:
                    if src == "meta":
                        continue
                    io_i[0] ^= 1
                    xo = io_i[0] * D
                    for c0 in range(0, KD, 4):
                        bk = bank()
                        nch = min(4, KD - c0)
                        for c in range(c0, c0 + nch):
                            o = ps.r(bk + (c - c0) * 128, bk + (c - c0 + 1) * 128, 0, n)
                            i_ = h.r(c * TM + t0, c * TM + t0 + n)
                            idr = ident.r(0, 128)
                            P.pe_group([lambda e, o=o, i_=i_, idr=idr: e.transpose(o.ap, i_.ap, idr.ap)], [i_, idr], [o])
                        pr = ps.r(bk, bk + nch * 128, 0, n)
                        orr = iob.r(xo + c0 * 128, xo + (c0 + nch) * 128, 0, n)
                        P.op("act", lambda e, orr=orr, pr=pr: e.activation(orr.ap, pr.ap, AF.Identity), [pr], [orr])
                    orr = iob.r(xo, xo + D, 0, n)
                    dst = out_d[sq, src:src + n, :]
                    P.dma("sp", lambda e, orr=orr, dst=dst: e.dma_start(out=dst, in_=orr.ap), [iob.r(xo, xo + D)], [], t_os[io_i[0]])
        P.finish(t_os)
        plan_out = {"descs": wl["plan"], "last_use": dict(P.last_use)}
    return nc, plan_out


class _CSBuf:
    def __init__(s, buf, base):
        s.buf, s.base = buf, base

    def r(s, a, b, p0=0, p1=128):
        return s.buf.r(s.base + a, s.base + b, p0, p1)


def _t5_bucket(d):
    d = np.asarray(d)
    n = np.maximum(d, 0)
    nf = np.maximum(n, 1).astype(np.float32)
    large = 16 + (np.log(nf / np.float32(16)) / np.float32(math.log(128 / 16)) * np.float32(16)).astype(np.int32)
    large = np.minimum(large, 31)
    return np.where(n < 16, n, large)


def _bias_tables(rel_bias):
    bg = np.zeros((5, 128, 2, 4, 128), np.float32)
    mk = np.zeros((5, 128, 2, 4, 128), np.float32)
    k = np.arange(128)[:, None]
    q = np.arange(128)[None, :]
    heads = (np.arange(2)[:, None] * 4 + np.array([0, 2, 1, 3])[None, :])
    def put(i, dist, valid, rows, ncols):
        b = _t5_bucket(dist)
        g = rel_bias[b][:, :, heads]
        bg[i, :rows, :, :, :ncols] = np.transpose(g, (0, 2, 3, 1))
        m = np.where(valid, 0.0, NEG).astype(np.float32)
        mk[i, :rows, :, :, :ncols] = m[:, None, None, :]
    put(0, q - k, q >= k, 128, 128)
    put(1, 128 + q - k, k > q, 128, 128)
    m16 = np.arange(16)[:, None]
    put(2, 16 + q - m16, np.ones((16, 128), bool), 16, 128)
    put(3, 16 + 128 + q - m16, np.ones((16, 128), bool), 16, 128)
    q16 = np.arange(16)[None, :]
    b = _t5_bucket(q16 - m16)
    g = rel_bias[b][:, :, heads]
    t = np.transpose(g, (0, 2, 3, 1))
    m = np.where(q16 >= m16, 0.0, NEG).astype(np.float32)
    bg4 = np.zeros((128, 2, 512), np.float32)
    mk4 = np.zeros((128, 2, 512), np.float32)
    bg4[:16, :, :64] = t.reshape(16, 2, 64)
    mk4[:16, :, :64] = np.broadcast_to(m[:, None, None, :], (16, 2, 4, 16)).reshape(16, 2, 64)
    bgf = bg.reshape(5, 128, 1024).copy()
    mkf = mk.reshape(5, 128, 1024).copy()
    bgf[4] = bg4.reshape(128, 1024)
    mkf[4] = mk4.reshape(128, 1024)
    return (np.ascontiguousarray(np.transpose(bgf, (1, 0, 2)).reshape(128, 5 * 1024)),
            np.ascontiguousarray(np.transpose(mkf, (1, 0, 2)).reshape(128, 5 * 1024)))


def _cols(v):
    v = np.asarray(v, np.float32)
    return np.ascontiguousarray(v.reshape(-1, 128).T)


def prep_inputs(C, inp):
    L, KD, KC, KF = C.L, C.KD, C.KC, C.KF
    f = lambda a: np.ascontiguousarray(np.asarray(a, np.float32))
    pvec = np.zeros((L, 128, C.NV), np.float32)
    z64 = np.zeros(64, np.float32)
    CW, D = C.CW, C.D
    for l in range(L):
        b = f(inp["b_in"][l])
        pv = pvec[l]
        pv[:, C.c_bq:C.c_bq + 4] = _cols(b[0:512])
        for hk in range(2):
            bk = b[512 + hk * 64: 512 + hk * 64 + 64]
            pv[:, C.c_bk + hk * 2 + 0] = np.concatenate([bk, z64])
            pv[:, C.c_bk + hk * 2 + 1] = np.concatenate([z64, bk])
        pv[:, C.c_bcv:C.c_bcv + KC] = _cols(b[768:768 + CW])
        pv[:, C.c_bcg:C.c_bcg + KC] = _cols(b[768 + CW:768 + 2 * CW])
        pv[:, C.c_bga:C.c_bga + KD] = _cols(b[768 + 2 * CW:768 + 2 * CW + D])
        pv[:, C.c_bgc:C.c_bgc + KD] = _cols(b[768 + 2 * CW + D:768 + 2 * CW + 2 * D])
        cdw = f(inp["conv_dw"][l])
        for tap in range(31):
            pv[:, C.c_cdw + tap * KC: C.c_cdw + (tap + 1) * KC] = _cols(cdw[tap])
        pv[:, C.c_cdb:C.c_cdb + KC] = _cols(inp["conv_dw_b"][l])
        pv[:, C.c_clg:C.c_clg + KC] = _cols(inp["conv_ln_g"][l])
        pv[:, C.c_clb:C.c_clb + KC] = _cols(inp["conv_ln_b"][l])
        pv[:, C.c_l1g:C.c_l1g + KD] = _cols(inp["ln1_g"][l])
        pv[:, C.c_l1b:C.c_l1b + KD] = _cols(inp["ln1_b"][l])
        fdw = f(inp["ffn_dw"][l])
        for tap in range(3):
            pv[:, C.c_fdw + tap * 2 * KF: C.c_fdw + (tap + 1) * 2 * KF] = _cols(fdw[tap])
        pv[:, C.c_fdb:C.c_fdb + 2 * KF] = _cols(inp["ffn_dw_b"][l])
        pv[:, C.c_l2g:C.c_l2g + KD] = _cols(inp["ln2_g"][l])
        pv[:, C.c_l2b:C.c_l2b + KD] = _cols(inp["ln2_b"][l])
    pv0 = np.concatenate([_cols(inp["in_ln_g"]), _cols(inp["in_ln_b"])], axis=1)
    bv = np.stack([np.broadcast_to(f(inp["b_in"][l])[640:768][None, :], (128, 128)) for l in range(L)])
    sinks = f(inp["attn_sinks"])
    p = np.arange(128)
    sl = np.zeros((L, 128, 2, 2, 128), np.float32)
    for l in range(L):
        for hk in range(2):
            for gp in range(2):
                hd = 4 * hk + 2 * gp + (p >= 64).astype(np.int64)
                sl[l, :, hk, gp, :] = sinks[l][hd][:, None]
    bg, mk = _bias_tables(f(inp["rel_bias"]))
    shared = {
        "meta": f(inp["meta_tokens"]), "w_in": f(inp["w_in"]), "w_ap": f(inp["w_attn_proj"]),
        "w_cp": f(inp["w_conv_proj"]), "w_out": f(inp["w_out"]), "w_up": f(inp["ffn_w_up"]),
        "w_dn": f(inp["ffn_w_down"]), "pvec": pvec, "pv0": np.ascontiguousarray(pv0),
        "bv": np.ascontiguousarray(bv), "sinkl": np.ascontiguousarray(sl.reshape(L, 128, 512)),
        "bias_g": bg, "mask_c": mk, "ident": np.eye(128, dtype=np.float32),
    }
    return shared


def kernel(**inputs):
    C = Cfg()
    x = np.ascontiguousarray(np.asarray(inputs["x"], np.float32))
    shared = prep_inputs(C, inputs)
    ncores = 8
    _, plan = build(C)
    nc, _ = build(C, plan=plan)
    in_maps = []
    for i in range(ncores):
        m = dict(shared)
        m["x"] = np.ascontiguousarray(x[i * C.NSEQ:(i + 1) * C.NSEQ])
        in_maps.append(m)
    res = run_bass_kernel_spmd(nc, in_maps, core_ids=list(range(ncores)))
    out = np.concatenate([np.asarray(r["out"]) for r in res.results], axis=0)
    return out.astype(np.float32)
```

```python
import contextlib, bisect, math
from collections import namedtuple
import numpy as np
import concourse.bass as bass
import concourse.mybir as mybir
from concourse.bass_utils import run_bass_kernel_spmd

F32, BF16 = mybir.dt.float32, mybir.dt.bfloat16
AF = mybir.ActivationFunctionType
ALU = mybir.AluOpType
Ref = namedtuple("Ref", "ap reg")
NEG = -30000.0
LNVAR = 2
LN_EPS = 1e-5


class Cfg:
    def __init__(s, D=1024, SEQ=2048, DFF=2816, NU=4, L=2, NSEQ=2):
        s.D, s.SEQ, s.DFF, s.NU, s.L, s.NSEQ = D, SEQ, DFF, NU, L, NSEQ
        s.KD = D // 128
        s.CW = D // 2
        s.KC = s.CW // 128
        s.KF = DFF // 128
        s.AW, s.KVW = 512, 128
        s.IN_COLS = s.AW + 2 * s.KVW + 2 * s.CW + 2 * D
        s.Sh = SEQ // NU
        s.TM = s.Sh + 16
        s.NB = s.Sh // 128
        s.alpha = (2.0 * L) ** 0.25
        o = 0
        def take(n):
            nonlocal o
            r = o; o += n; return r
        s.c_bq = take(4); s.c_bk = take(4); s.c_bcv = take(s.KC); s.c_bcg = take(s.KC)
        s.c_bga = take(s.KD); s.c_bgc = take(s.KD)
        s.c_cdw = take(31 * s.KC); s.c_cdb = take(s.KC); s.c_clg = take(s.KC); s.c_clb = take(s.KC)
        s.c_l1g = take(s.KD); s.c_l1b = take(s.KD)
        s.c_fdw = take(3 * 2 * s.KF); s.c_fdb = take(2 * s.KF)
        s.c_l2g = take(s.KD); s.c_l2b = take(s.KD)
        s.c_bq8 = take(4)
        s.NV = o


def split_tiles(n, maxw=512):
    nt = -(-n // maxw)
    w = -(-n // nt)
    w = -(-w // 16) * 16
    res, t = [], 0
    while t < n:
        ww = min(w, n - t)
        res.append((t, ww))
        t += ww
    return res


class TL:
    def __init__(s, sem, step):
        s.sem, s.step, s.val = sem, step, 0


class Buf:
    def __init__(s, P, name, width, dt, space="sbuf"):
        s.name, s.width, s.dt = name, width, dt
        if space == "sbuf":
            s.h = P.st.enter_context(P.nc.sbuf_tensor("t_" + name, [128, width], dt))
        else:
            s.h = P.st.enter_context(P.nc.psum_tensor("t_" + name, [128, width], dt))
        P.mem[name] = [[0, width, None, {}]]
        P.starts[name] = [0]

    def r(s, a, b, p0=0, p1=128):
        assert 0 <= a < b <= s.width, (s.name, a, b, s.width)
        return Ref(s.h[p0:p1, a:b], (s.name, a, b))

    def r3(s, base, n_outer, stride, inner, p0=0, p1=128):
        end = base + (n_outer - 1) * stride + inner
        assert end <= s.width, (s.name, base, n_outer, stride, inner)
        if n_outer == 1:
            return s.r(base, base + inner, p0, p1)
        rb = min(base, s.width - n_outer * stride)
        assert rb >= 0, (s.name, base, n_outer, stride, inner, s.width)
        off = base - rb
        assert off + inner <= stride, (s.name, base, n_outer, stride, inner, s.width)
        ap = s.h[p0:p1, rb:rb + n_outer * stride].rearrange("p (o s) -> p o s", s=stride)[:, :, off:off + inner]
        return Ref(ap, (s.name, base, end))


class Prog:
    ENGS = ("pe", "act", "dve", "pool", "sp")

    def __init__(s, nc, st):
        s.nc, s.st = nc, st
        s.ops = {e: [] for e in s.ENGS}
        s.known = {e: {} for e in s.ENGS}
        s.mem, s.starts = {}, {}
        s.tl = {}
        for e in ("pe", "act", "dve", "pool"):
            s.tl[e] = TL(st.enter_context(nc.semaphore("s_" + e)), 1)
        s.ndma = 0
        s.cnt = 0
        s.hook = None
        s.pending_w = set()
        s.last_use = {}

    def dma_tl(s, name):
        return TL(s.st.enter_context(s.nc.semaphore("d_" + name)), 16)

    def _touch(s, reg, stamp, is_write, needs):
        name, lo, hi = reg
        segs, starts = s.mem[name], s.starts[name]
        i = bisect.bisect_right(starts, lo) - 1
        j = i
        new = []
        wrote = False
        while j < len(segs) and segs[j][0] < hi:
            a, b, w, r = segs[j]
            if w is not None and w != stamp:
                needs.append(w)
            if is_write:
                for t, v in r.items():
                    if (t, v) != stamp:
                        needs.append((t, v))
            if a < lo:
                new.append([a, lo, w, dict(r)])
            if is_write:
                if not wrote:
                    new.append([lo, hi, stamp, {}])
                    wrote = True
            else:
                r2 = dict(r)
                r2[stamp[0]] = max(r2.get(stamp[0], 0), stamp[1])
                new.append([max(a, lo), min(b, hi), w, r2])
            if b > hi:
                new.append([hi, b, w, dict(r)])
            j += 1
        segs[i:j] = new
        starts[i:j] = [x[0] for x in new]

    def _emit(s, eng, fns, reads, writes, tl, inc, is_wload=False):
        if not is_wload:
            if s.hook is not None:
                s.hook(s.cnt)
            for k in s.pending_w:
                s.last_use[k] = s.cnt
            s.pending_w.clear()
            s.cnt += 1
        stamp = (tl, tl.val + inc)
        needs = []
        for reg in reads:
            if reg is not None:
                s._touch(reg, stamp, False, needs)
        for reg in writes:
            if reg is not None:
                s._touch(reg, stamp, True, needs)
        tl.val += inc
        kn = s.known[eng]
        waits = {}
        for t, v in needs:
            if kn.get(t, 0) < v:
                waits[t] = max(waits.get(t, 0), v)
        for t, v in waits.items():
            kn[t] = v
        s.ops[eng].append(([(t.sem, v) for t, v in waits.items()], fns, (tl.sem, inc)))

    def op(s, eng, fn, reads, writes):
        s._emit(eng, [fn], [r.reg if isinstance(r, Ref) else r for r in reads],
                [w.reg if isinstance(w, Ref) else w for w in writes], s.tl[eng], 1)

    def pe_group(s, fns, reads, writes):
        s._emit("pe", fns, [r.reg for r in reads], [w.reg for w in writes], s.tl["pe"], 1)

    def dma(s, eng, fn, reads, writes, tl, is_wload=False):
        s._emit(eng, [fn], [r.reg if isinstance(r, Ref) else r for r in reads],
                [w.reg if isinstance(w, Ref) else w for w in writes], tl, 16, is_wload)

    def op_w(s, eng, fn, reads, writes):
        s._emit(eng, [fn], [r.reg if isinstance(r, Ref) else r for r in reads],
                [w.reg if isinstance(w, Ref) else w for w in writes], s.tl[eng], 1, True)

    def wait(s, eng, tl):
        if s.known[eng].get(tl, 0) < tl.val:
            s.known[eng][tl] = tl.val
            s.ops[eng].append(([(tl.sem, tl.val)], [], None))

    def finish(s, final_tls):
        nc = s.nc
        engmap = {"pe": "tensor", "act": "scalar", "dve": "vector", "pool": "gpsimd", "sp": "sync"}
        with nc.Block() as block:
            for e in s.ENGS:
                ops = s.ops[e]

                def body(eng, ops=ops, e=e):
                    for waits, fns, sig in ops:
                        for sem, v in waits:
                            eng.wait_ge(sem, v)
                        ins = None
                        for fn in fns:
                            ins = fn(eng)
                        if ins is not None:
                            ins.then_inc(sig[0], sig[1])
                    if e == "sp":
                        for t in final_tls:
                            eng.wait_ge(t.sem, t.val)
                getattr(block, engmap[e])(body)


def build(cfg, dbg=None, plan=None):
    C = cfg
    D, KD, KC, KF, TM, L = C.D, C.KD, C.KC, C.KF, C.TM, C.L
    nc = bass.Bass("TRN2", target_bir_lowering=False)
    dt_in = lambda name, shape: nc.dram_tensor(name, list(shape), F32, kind="ExternalInput").ap()
    x_d = dt_in("x", [C.NSEQ, C.SEQ, D])
    meta_d = dt_in("meta", [16, D])
    w_in_d = dt_in("w_in", [L, D, C.IN_COLS])
    w_ap_d = dt_in("w_ap", [L, C.AW, D])
    w_cp_d = dt_in("w_cp", [L, C.CW, D])
    w_out_d = dt_in("w_out", [L, D, D])
    w_up_d = dt_in("w_up", [L, D, 2 * C.DFF])
    w_dn_d = dt_in("w_dn", [L, C.DFF, D])
    pvec_d = dt_in("pvec", [L, 128, C.NV])
    pv0_d = dt_in("pv0", [128, 2 * KD])
    bv_d = dt_in("bv", [L, 128, 128])
    sink_d = dt_in("sinkl", [L, 128, 512])
    bg_d = dt_in("bias_g", [128, 5 * 1024])
    mk_d = dt_in("mask_c", [128, 5 * 1024])
    id_d = dt_in("ident", [128, 128])
    out_d = nc.dram_tensor("out", [C.NSEQ, C.SEQ, D], F32, kind="ExternalOutput").ap()

    st = contextlib.ExitStack()
    with st:
        P = Prog(nc, st)
        B = lambda name, w, dt=BF16: Buf(P, name, w, dt)
        ps = Buf(P, "ps", 4096, F32, space="psum")
        h = B("h", KD * TM, F32)
        hb = B("hb", KD * TM)
        o_q = 0
        o_kd = o_q + 4 * TM
        o_c = o_kd + 4 * TM
        CS = 32 + TM
        o_vt = o_c + KC * CS
        o_cs = o_vt + (C.NB + 1) * 384
        o_a = o_cs + KC * TM
        o_end = o_a + 4 * TM
        assert KD * TM <= o_cs
        AW_ = max(o_end, KF * TM)
        ar = B("ar", AW_)
        ycv = B("ycv", KC * TM, F32)
        stgb = ycv if KC * TM >= 2048 else B("stg", 2048, F32)
        SW = 2048
        NSLOT = 14
        wb = B("wb", NSLOT * SW)
        slot_tl = [P.dma_tl("w%d" % i) for i in range(NSLOT)]
        yb = B("yb", KD * TM)
        ysq = B("ysq", KD * TM)
        mean_sb = B("mean_sb", 512, F32)
        m2 = B("m2", 512, F32)
        rstd = B("rstd", 512, F32)
        tmpA = B("tmpA", 512, F32)
        tmpB = B("tmpB", 512, F32)
        ub = B("ub", 2 * (16 + TM))
        gb = B("gb", 2 * (16 + TM))
        glb = B("glb", 2 * 512, F32)
        pt = B("pt", 2 * 3 * 512)
        dsb = B("dsb", 2 * 256, F32)
        iob = B("iob", 2 * D, F32)
        io_i = [0]
        pvec = B("pvec", L * C.NV, F32)
        pv0 = B("pv0", 2 * KD, F32)
        bvb = B("bvb", L * 128, F32)
        esink = B("esink", L * 512, F32)
        bm = B("bm", 5 * 1024)
        ident = B("ident", 128, F32)
        identb = B("identb", 128)
        onesz = B("onesz", 192)
        onesD = B("onesD", 128)
        onesC = B("onesC", 128)
        dg = B("dg", 64 * 128)
        dgset = [0]
        kcar = B("kcar", L * 4 * 144)
        vcar = B("vcar", L * 2 * 384)
        ccar = B("ccar", L * KC * 32)
        ucar = B("ucar", L * 2 * KF * 2)

        t_c = P.dma_tl("const")
        t_xs = [P.dma_tl("x0"), P.dma_tl("x1")]
        t_os = [P.dma_tl("o0"), P.dma_tl("o1")]

        bank_i = [0]

        def bank():
            b = bank_i[0]
            bank_i[0] = (b + 1) % 8
            return b * 512

        dram2d = {"w_in": w_in_d, "w_ap": w_ap_d, "w_cp": w_cp_d, "w_out": w_out_d, "w_up": w_up_d, "w_dn": w_dn_d}
        wl = {"req": 0, "issued": 0, "plan": [] if plan is None else plan["descs"]}
        prev_last = None if plan is None else plan["last_use"]

        def w_issue(i):
            d = wl["plan"][i]
            sl = i % NSLOT
            base = sl * SW
            if d[0] == "std":
                _, nm, l_, r0, kc, c0, ncols = d
                dst = wb.h[:, base:base + kc * ncols].rearrange("p (k c) -> p k c", k=kc)
                src = dram2d[nm][l_][r0:r0 + kc * 128, c0:c0 + ncols].rearrange("(k p) c -> p k c", p=128)
                P.dma("pool", lambda e: e.dma_start(out=dst, in_=src), [], [("wb", base, base + kc * ncols)], slot_tl[sl], True)
            else:
                _, l_, half = d
                wz = wb.r(base, base + KD * 256)
                P.op_w("pool", lambda e, wz=wz: e.memset(wz.ap, 0.0), [], [wz])
                hk = half
                for hf in range(2):
                    dst = wb.h[:, base:base + KD * 256].rearrange("p (k c) -> p k c", k=KD)[:, :, hf * 128 + hf * 64: hf * 128 + hf * 64 + 64]
                    src = w_in_d[l_][:, 512 + hk * 64: 512 + hk * 64 + 64].rearrange("(k p) c -> p k c", p=128)
                    P.dma("pool", lambda e, dst=dst, src=src: e.dma_start(out=dst, in_=src), [], [wz], slot_tl[sl], True)
            wl["issued"] = i + 1

        def w_pump(t):
            if plan is None:
                return
            while wl["issued"] < len(wl["plan"]):
                i = wl["issued"]
                if i >= NSLOT and not (prev_last.get(i - NSLOT, -1) < t):
                    break
                w_issue(i)

        P.hook = w_pump

        def w_request(desc):
            k = wl["req"]
            wl["req"] = k + 1
            if plan is None:
                wl["plan"].append(desc)
            else:
                assert wl["plan"][k] == desc, (k, wl["plan"][k], desc)
            while wl["issued"] <= k:
                w_issue(wl["issued"])
            base = (k % NSLOT) * SW
            ncols = desc[6] if desc[0] == "std" else 256

            def acc(kk, ca, cb, k=k, base=base, ncols=ncols):
                P.pending_w.add(k)
                return wb.r(base + kk * ncols + ca, base + kk * ncols + cb)
            return acc

        def load_w(nm, l_, r0, kc, c0, ncols):
            if kc * ncols <= SW:
                return w_request(("std", nm, l_, r0, kc, c0, ncols))
            assert ncols % 256 == 0 and kc * 256 <= SW
            subs = [w_request(("std", nm, l_, r0, kc, c0 + cc, 256)) for cc in range(0, ncols, 256)]

            def acc(kk, ca, cb):
                si = ca // 256
                assert (cb - 1) // 256 == si
                return subs[si](kk, ca - si * 256, cb - si * 256)
            return acc

        def load_wk(l_):
            subs = [w_request(("k", l_, half)) for half in range(2)]

            def acc(kk, ca, cb):
                si = ca // 256
                assert (cb - 1) // 256 == si
                return subs[si](kk, ca - si * 256, cb - si * 256)
            return acc

        def _f_A(o, l, r, a, b): return lambda e: e.matmul(o.ap, l.ap, r.ap, start=a, stop=b)
        def _f_B(o, l, r, a, b): return lambda e: e.matmul(o.ap, l.ap, r.ap, start=a, stop=b)
        def _f_L(o, l, r, a, b): return lambda e: e.matmul(o.ap, l.ap, r.ap, start=a, stop=b)
        def _f_C(o, l, r, a, b): return lambda e: e.matmul(o.ap, l.ap, r.ap, start=a, stop=b)
        def _f_D(o, l, r, a, b): return lambda e: e.matmul(o.ap, l.ap, r.ap, start=a, stop=b)
        def _f_E(o, l, r, a, b): return lambda e: e.matmul(o.ap, l.ap, r.ap, start=a, stop=b)
        def _f_U(o, l, r, a, b): return lambda e: e.matmul(o.ap, l.ap, r.ap, start=a, stop=b)
        def _f_V(o, l, r, a, b): return lambda e: e.matmul(o.ap, l.ap, r.ap, start=a, stop=b)
        def _f_W(o, l, r, a, b): return lambda e: e.matmul(o.ap, l.ap, r.ap, start=a, stop=b)
        facs = {"A": _f_A, "B": _f_B, "L": _f_L, "C": _f_C, "D": _f_D, "E": _f_E, "U": _f_U, "V": _f_V, "W": _f_W}

        def mm(out, pairs, tag="A", first=True, last=True):
            n = len(pairs)
            fac = facs[tag]
            fns = [fac(out, l, r, first and i == 0, last and i == n - 1) for i, (l, r) in enumerate(pairs)]
            P.pe_group(fns, [x for p in pairs for x in p], [out])

        def pcol(l, c):
            return pvec.r(l * C.NV + c, l * C.NV + c + 1)

        def cdma(dst, src):
            P.dma("sp", lambda e: e.dma_start(out=dst.ap, in_=src), [], [dst], t_c)
            P.wait("sp", t_c)

        cdma(ident.r(0, 128), id_d)
        cdma(pv0.r(0, 2 * KD), pv0_d)
        for l in range(L):
            cdma(pvec.r(l * C.NV, (l + 1) * C.NV), pvec_d[l])
            cdma(bvb.r(l * 128, (l + 1) * 128), bv_d[l])
            cdma(esink.r(l * 512, (l + 1) * 512), sink_d[l])
        P.op("dve", lambda e: e.tensor_copy(identb.r(0, 128).ap, ident.r(0, 128).ap), [ident.r(0, 128)], [identb.r(0, 128)])
        P.op("dve", lambda e: e.memset(onesz.r(0, 192).ap, 0.0), [], [onesz.r(0, 192)])
        P.op("dve", lambda e: e.memset(onesz.r(64, 128).ap, 1.0), [], [onesz.r(64, 128)])
        P.op("dve", lambda e: e.memset(onesD.r(0, 128).ap, 1.0 / D), [], [onesD.r(0, 128)])
        P.op("dve", lambda e: e.memset(onesC.r(0, 128).ap, 1.0 / C.CW), [], [onesC.r(0, 128)])
        P.op("dve", lambda e: e.memset(ar.r(0, AW_).ap, 0.0), [], [ar.r(0, AW_)])
        P.op("dve", lambda e: e.memset(vcar.r(0, L * 768).ap, 0.0), [], [vcar.r(0, L * 768)])
        P.op("dve", lambda e: e.memset(ub.r(0, 2 * (16 + TM)).ap, 0.0), [], [ub.r(0, 2 * (16 + TM))])
        P.op("dve", lambda e: e.memset(gb.r(0, 2 * (16 + TM)).ap, 0.0), [], [gb.r(0, 2 * (16 + TM))])
        for i in range(5):
            s1, s2 = stgb.r(0, 1024), stgb.r(1024, 2048)
            cdma(s1, bg_d[:, i * 1024:(i + 1) * 1024])
            cdma(s2, mk_d[:, i * 1024:(i + 1) * 1024])
            o = bm.r(i * 1024, (i + 1) * 1024)
            P.op("dve", lambda e, o=o, s1=s1, s2=s2: e.tensor_tensor(o.ap, s1.ap, s2.ap, ALU.add), [s1, s2], [o])
        for l in range(L):
            o = esink.r(l * 512, (l + 1) * 512)
            P.op("act", lambda e, o=o: e.activation(o.ap, o.ap, AF.Exp), [o], [o])
            o8 = pvec.r(l * C.NV + C.c_bq8, l * C.NV + C.c_bq8 + 4)
            i8 = pvec.r(l * C.NV + C.c_bq, l * C.NV + C.c_bq + 4)
            P.op("dve", lambda e, o8=o8, i8=i8: e.tensor_scalar(o8.ap, i8.ap, 0.125, None, ALU.mult), [i8], [o8])

        def ln_pre(ybuf, ystride, c, t0, n):
            yc = ybuf.r(c * ystride + t0, c * ystride + t0 + n)
            ybc = yb.r(c * TM + t0, c * TM + t0 + n)
            ysc = ysq.r(c * TM + t0, c * TM + t0 + n)
            P.op("act", lambda e: e.activation(ybc.ap, yc.ap, AF.Identity), [yc], [ybc])
            P.op("act", lambda e: e.activation(ysc.ap, yc.ap, AF.Square), [yc], [ysc])

        def ln_stats_part1(nch, ones_ref, tiles_):
            res = {}
            assert 3 * len(tiles_) <= 7
            for (t0, n) in tiles_:
                pm, pq = bank(), bank()
                pmr, pqr = ps.r(pm, pm + n), ps.r(pq, pq + n)
                mm(pmr, [(ones_ref, yb.r(c * TM + t0, c * TM + t0 + n)) for c in range(nch - 1)], "L", True, False)
                mm(pqr, [(ones_ref, ysq.r(c * TM + t0, c * TM + t0 + n)) for c in range(nch - 1)], "L", True, False)
                res[t0] = (pm, pq)
            return res

        def layer_norm(ybuf, ystride, nch, ones_ref, t0, n, gcol, bcol, outs, part1=None, defer=None):
            if part1 is None:
                pm = bank()
                pq = bank()
                pmr, pqr = ps.r(pm, pm + n), ps.r(pq, pq + n)
                mm(pmr, [(ones_ref, yb.r(c * TM + t0, c * TM + t0 + n)) for c in range(nch)], "L")
                mm(pqr, [(ones_ref, ysq.r(c * TM + t0, c * TM + t0 + n)) for c in range(nch)], "L")
            else:
                pm, pq = part1[t0]
                pmr, pqr = ps.r(pm, pm + n), ps.r(pq, pq + n)
                c = nch - 1
                mm(pmr, [(ones_ref, yb.r(c * TM + t0, c * TM + t0 + n))], "L", False, True)
                mm(pqr, [(ones_ref, ysq.r(c * TM + t0, c * TM + t0 + n))], "L", False, True)
            ms, m2r, rs = mean_sb.r(0, n), m2.r(0, n), rstd.r(0, n)
            P.op("dve", lambda e: e.tensor_copy(ms.ap, pmr.ap), [pmr], [ms])
            if LNVAR & 1:
                P.op("act", lambda e: e.activation(m2r.ap, pmr.ap, AF.Square), [pmr], [m2r])
            else:
                P.op("dve", lambda e: e.tensor_tensor(m2r.ap, ms.ap, ms.ap, ALU.mult), [ms], [m2r])
            if LNVAR & 2:
                P.op("dve", lambda e: e.scalar_tensor_tensor(m2r.ap, pqr.ap, LN_EPS, m2r.ap, ALU.add, ALU.subtract), [pqr, m2r], [m2r])
            else:
                P.op("dve", lambda e: e.tensor_tensor(m2r.ap, pqr.ap, m2r.ap, ALU.subtract), [pqr, m2r], [m2r])
                P.op("dve", lambda e: e.tensor_scalar(m2r.ap, m2r.ap, LN_EPS, None, ALU.add), [m2r], [m2r])
            P.op("act", lambda e: e.activation(rs.ap, m2r.ap, AF.Ln), [m2r], [rs])
            P.op("act", lambda e: e.activation(rs.ap, rs.ap, AF.Exp, scale=-0.5), [rs], [rs])
            def grp(c0, gn):
                y3 = ybuf.r3(c0 * ystride + t0, gn, ystride, n)
                return y3
            GN = 2
            for c0 in range(0, nch, GN):
                gn = min(GN, nch - c0)
                y3 = grp(c0, gn)
                msb = ms.ap if gn == 1 else ms.ap.unsqueeze(1).broadcast_to([128, gn, n])
                P.op("dve", lambda e, y3=y3, msb=msb: e.tensor_tensor(y3.ap, y3.ap, msb, ALU.subtract), [y3, ms], [y3])
            if defer is not None:
                for c0 in range(0, nch, GN):
                    gn = min(GN, nch - c0)
                    y3 = grp(c0, gn)
                    rsb = rs.ap if gn == 1 else rs.ap.unsqueeze(1).broadcast_to([128, gn, n])
                    P.op("dve", lambda e, y3=y3, rsb=rsb: e.tensor_tensor(y3.ap, y3.ap, rsb, ALU.mult), [y3, rs], [y3])

                def emit_outs():
                    for c in range(nch):
                        yc = ybuf.r(c * ystride + t0, c * ystride + t0 + n)
                        for (buf, stride, func) in outs:
                            oc = buf.r(c * stride + t0, c * stride + t0 + n)
                            bc_, gc_ = bcol(c), gcol(c)
                            P.op("act", lambda e, oc=oc, yc=yc, func=func, bc_=bc_, gc_=gc_: e.activation(
                                oc.ap, yc.ap, func, bias=bc_.ap, scale=gc_.ap), [yc, gc_, bc_], [oc])
                defer.append(emit_outs)
                return
            for c0 in range(0, nch, GN):
                gn = min(GN, nch - c0)
                y3 = grp(c0, gn)
                rsb = rs.ap if gn == 1 else rs.ap.unsqueeze(1).broadcast_to([128, gn, n])
                P.op("dve", lambda e, y3=y3, rsb=rsb: e.tensor_tensor(y3.ap, y3.ap, rsb, ALU.mult), [y3, rs], [y3])
                for c in range(c0, c0 + gn):
                    yc = ybuf.r(c * ystride + t0, c * ystride + t0 + n)
                    for (buf, stride, func) in outs[:1]:
                        oc = buf.r(c * stride + t0, c * stride + t0 + n)
                        bc_, gc_ = bcol(c), gcol(c)
                        P.op("act", lambda e, oc=oc, yc=yc, func=func, bc_=bc_, gc_=gc_: e.activation(
                            oc.ap, yc.ap, func, bias=bc_.ap, scale=gc_.ap), [yc, gc_, bc_], [oc])
            for c in range(nch):
                yc = ybuf.r(c * ystride + t0, c * ystride + t0 + n)
                for (buf, stride, func) in outs[1:]:
                    oc = buf.r(c * stride + t0, c * stride + t0 + n)
                    bc_, gc_ = bcol(c), gcol(c)
                    P.op("act", lambda e, oc=oc, yc=yc, func=func, bc_=bc_, gc_=gc_: e.activation(
                        oc.ap, yc.ap, func, bias=bc_.ap, scale=gc_.ap), [yc, gc_, bc_], [oc])

        def h_ln(gcol, bcol, t0, n, part1=None, need_hb=True):
            outs = [(hb, TM, AF.Identity), (h, TM, AF.Identity)] if need_hb else [(h, TM, AF.Identity)]
            layer_norm(h, TM, KD, onesD.r(0, 128), t0, n, gcol, bcol, outs, part1)

        for sq in range(C.NSEQ):
            for u in range(C.NU):
                off0 = 16 if u == 0 else 0
                Tu = C.Sh + off0
                tiles = split_tiles(Tu)
                blocks = []
                if u == 0:
                    blocks.append((0, 16, "meta"))
                for i in range(C.NB):
                    blocks.append((off0 + 128 * i, 128, u * C.Sh + 128 * i))
                for (t0, n, src) in blocks:
                    io_i[0] ^= 1
                    xo = io_i[0] * D
                    xr = iob.r(xo, xo + D, 0, n)
                    srcap = meta_d if src == "meta" else x_d[sq, src:src + n, :]
                    P.dma("sp", lambda e, xr=xr, srcap=srcap: e.dma_start(out=xr.ap, in_=srcap), [], [iob.r(xo, xo + D)], t_xs[io_i[0]])
                    for c0 in range(0, KD, 4):
                        bk = bank()
                        for c in range(c0, min(c0 + 4, KD)):
                            o = ps.r(bk + (c - c0) * 128, bk + (c - c0) * 128 + n)
                            i_ = iob.r(xo + c * 128, xo + (c + 1) * 128, 0, n)
                            idr = ident.r(0, n, 0, n)
                            P.pe_group([lambda e, o=o, i_=i_, idr=idr: e.transpose(o.ap, i_.ap, idr.ap)], [i_, idr], [o])
                        nch = min(4, KD - c0)
                        src3 = ps.r3(bk, nch, 128, n)
                        dst3 = h.r3(c0 * TM + t0, nch, TM, n)
                        P.op("dve", lambda e, src3=src3, dst3=dst3: e.tensor_copy(dst3.ap, src3.ap), [src3], [dst3])
                        for c in range(c0, c0 + nch):
                            ln_pre(h, TM, c, t0, n)
                for (t0, n) in tiles:
                    h_ln(lambda c: pv0.r(c, c + 1), lambda c: pv0.r(KD + c, KD + c + 1), t0, n)

                for l in range(L if dbg is None else dbg):
                    NVl = l * C.NV
                    hst = [None]
                    dpar = [0]
                    hbr = lambda k, t0, n: hb.r(k * TM + t0, k * TM + t0 + n)
                    def conv_diags(j):
                        dbase = (j % 2) * 32 * 128
                        for tap in range(31):
                            dgr = dg.r(dbase + tap * 128, dbase + tap * 128 + 128)
                            wc_ = pcol(l, C.c_cdw + tap * KC + j)
                            P.op("dve", lambda e, dgr=dgr, wc_=wc_: e.tensor_scalar(dgr.ap, identb.r(0, 128).ap, wc_.ap, None, ALU.mult),
                                 [identb.r(0, 128), wc_], [dgr])

                    conv_diags(0)
                    wq = load_w("w_in", l, 0, KD, 0, 512)
                    for j in range(4):
                        for (t0, n) in tiles:
                            bk = bank()
                            pr = ps.r(bk, bk + n)
                            mm(pr, [(wq(k, j * 128, (j + 1) * 128), hbr(k, t0, n)) for k in range(KD)])
                            o = ar.r(o_q + j * TM + t0, o_q + j * TM + t0 + n)
                            bc = pcol(l, C.c_bq8 + j)
                            P.op("act", lambda e, o=o, pr=pr, bc=bc: e.activation(o.ap, pr.ap, AF.Identity, bias=bc.ap, scale=0.125), [pr, bc], [o])
                    wk = load_wk(l)
                    for v in range(4):
                        for (t0, n) in tiles:
                            bk = bank()
                            pr = ps.r(bk, bk + n)
                            mm(pr, [(wk(k, v * 128, (v + 1) * 128), hbr(k, t0, n)) for k in range(KD)])
                            o = ar.r(o_kd + v * TM + t0, o_kd + v * TM + t0 + n)
                            bc = pcol(l, C.c_bk + v)
                            P.op("act", lambda e, o=o, pr=pr, bc=bc: e.activation(o.ap, pr.ap, AF.Identity, bias=bc.ap, scale=1.0), [pr, bc], [o])
                    wv = load_w("w_in", l, 0, KD, 640, 128)
                    for bi, (t0, n, _) in enumerate(blocks):
                        bk = bank()
                        pr = ps.r(bk, bk + 128, 0, n)
                        mm(pr, [(hbr(k, t0, n), wv(k, 0, 128)) for k in range(KD)])
                        o = ar.r3(o_vt + bi * 384 + 64, 2, 192, 64, 0, n)
                        pr3 = ps.r3(bk, 2, 64, 64, 0, n)
                        bvr = bvb.r3(l * 128, 2, 64, 64, 0, n)
                        P.op("dve", lambda e, o=o, pr3=pr3, bvr=bvr: e.tensor_tensor(o.ap, pr3.ap, bvr.ap, ALU.add), [pr3, bvr], [o])
                    for j0 in range(0, KC, 4):
                        nj = min(4, KC - j0)
                        wcv = load_w("w_in", l, 0, KD, 768 + j0 * 128, nj * 128)
                        wcg = load_w("w_in", l, 0, KD, 768 + C.CW + j0 * 128, nj * 128)
                        for jj in range(nj):
                            j = j0 + jj
                            for (t0, n) in tiles:
                                b1, b2 = bank(), bank()
                                p1, p2 = ps.r(b1, b1 + n), ps.r(b2, b2 + n)
                                mm(p1, [(wcv(k, jj * 128, (jj + 1) * 128), hbr(k, t0, n)) for k in range(KD)])
                                mm(p2, [(wcg(k, jj * 128, (jj + 1) * 128), hbr(k, t0, n)) for k in range(KD)])
                                sg = tmpA.r(0, n)
                                bcg, bcv = pcol(l, C.c_bcg + j), pcol(l, C.c_bcv + j)
                                P.op("act", lambda e, sg=sg, p2=p2, bcg=bcg: e.activation(sg.ap, p2.ap, AF.Sigmoid, bias=bcg.ap, scale=1.0), [p2, bcg], [sg])
                                o = ar.r(o_c + j * CS + 32 + t0, o_c + j * CS + 32 + t0 + n)
                                P.op("dve", lambda e, o=o, p1=p1, bcv=bcv, sg=sg: e.scalar_tensor_tensor(
                                    o.ap, p1.ap, bcv.ap, sg.ap, ALU.add, ALU.mult), [p1, bcv, sg], [o])
                    kc_base = l * 4 * 144
                    vc_base = l * 768
                    cc_base = l * KC * 32
                    if u == 0:
                        for v in range(4):
                            s_ = ar.r(o_kd + v * TM, o_kd + v * TM + 16)
                            d_ = kcar.r(kc_base + v * 144, kc_base + v * 144 + 16)
                            P.op("pool", lambda e, s_=s_, d_=d_: e.tensor_copy(d_.ap, s_.ap), [s_], [d_])
                        s_ = ar.r(o_vt, o_vt + 384, 0, 16)
                        d_ = vcar.r(vc_base, vc_base + 384, 0, 16)
                        P.op("pool", lambda e, s_=s_, d_=d_: e.tensor_copy(d_.ap, s_.ap), [s_], [d_])
                        for j in range(KC):
                            z = ar.r(o_c + j * CS, o_c + j * CS + 32)
                            P.op("pool", lambda e, z=z: e.memset(z.ap, 0.0), [], [z])
                    else:
                        for j in range(KC):
                            z = ar.r(o_c + j * CS, o_c + j * CS + 32)
                            s_ = ccar.r(cc_base + j * 32, cc_base + j * 32 + 32)
                            P.op("pool", lambda e, z=z, s_=s_: e.tensor_copy(z.ap, s_.ap), [s_], [z])

                    for j in range(KC):
                        bks = [bank() for _ in tiles]
                        if j + 1 < KC:
                            conv_diags(j + 1)
                        dbase = (j % 2) * 32 * 128
                        for ti, (t0, n) in enumerate(tiles):
                            pr = ps.r(bks[ti], bks[ti] + n)
                            prs = []
                            for tap in range(31):
                                dgr = dg.r(dbase + tap * 128, dbase + tap * 128 + 128)
                                rb = o_c + j * CS + 2 + tap + t0
                                prs.append((dgr, ar.r(rb, rb + n)))
                            mm(pr, prs, "B")
                        for ti, (t0, n) in enumerate(tiles):
                            pr = ps.r(bks[ti], bks[ti] + n)
                            o = ycv.r(j * TM + t0, j * TM + t0 + n)
                            bc = pcol(l, C.c_cdb + j)
                            P.op("act", lambda e, o=o, pr=pr, bc=bc: e.activation(o.ap, pr.ap, AF.Identity, bias=bc.ap, scale=1.0), [pr, bc], [o])
                            ln_pre(ycv, TM, j, t0, n)
                    if u + 1 < C.NU:
                        for j in range(KC):
                            s_ = ar.r(o_c + j * CS + Tu, o_c + j * CS + Tu + 32)
                            d_ = ccar.r(cc_base + j * 32, cc_base + j * 32 + 32)
                            P.op("pool", lambda e, s_=s_, d_=d_: e.tensor_copy(d_.ap, s_.ap), [s_], [d_])
                    cl_defer = []

                    def conv_ln():
                        for (t0, n) in tiles:
                            layer_norm(ycv, TM, KC, onesC.r(0, 128), t0, n,
                                       lambda c: pcol(l, C.c_clg + c), lambda c: pcol(l, C.c_clb + c),
                                       [(_CSBuf(ar, o_cs), TM, AF.Silu)], defer=cl_defer)

                    att_items = []
                    for bi, (t0, nq, src) in enumerate(blocks):
                        kts = []
                        if src == "meta":
                            kts.append(("m0", 16, lambda v: ar.r(o_kd + v * TM, o_kd + v * TM + 16),
                                        lambda hk, par: ar.r(o_vt + hk * 192 + 64 - 64 * par, o_vt + hk * 192 + 192 - 64 * par, 0, 16), 4))
                        else:
                            first_of_seq = (u == 0 and bi == 1)
                            kts.append(("cur", 128, lambda v, t0=t0: ar.r(o_kd + v * TM + t0, o_kd + v * TM + t0 + 128),
                                        lambda hk, par, bi=bi: ar.r(o_vt + bi * 384 + hk * 192 + 64 - 64 * par, o_vt + bi * 384 + hk * 192 + 192 - 64 * par), 0))
                            if not first_of_seq:
                                if bi == 0:
                                    kts.append(("prev", 128, lambda v: kcar.r(kc_base + v * 144 + 16, kc_base + v * 144 + 144),
                                                lambda hk, par: vcar.r(vc_base + 384 + hk * 192 + 64 - 64 * par, vc_base + 384 + hk * 192 + 192 - 64 * par), 1))
                                else:
                                    kts.append(("prev", 128, lambda v, t0=t0: ar.r(o_kd + v * TM + t0 - 128, o_kd + v * TM + t0),
                                                lambda hk, par, bi=bi: ar.r(o_vt + (bi - 1) * 384 + hk * 192 + 64 - 64 * par, o_vt + (bi - 1) * 384 + hk * 192 + 192 - 64 * par), 1))
                            kts.append(("meta", 16, lambda v: kcar.r(kc_base + v * 144, kc_base + v * 144 + 16),
                                        lambda hk, par: vcar.r(vc_base + hk * 192 + 64 - 64 * par, vc_base + hk * 192 + 192 - 64 * par, 0, 16),
                                        2 if first_of_seq else 3))
                        for hk in range(2):
                            att_items.append((t0, nq, kts, hk))

                    def att_p1(idx, t0, nq, kts, hk):
                        pbase = (idx % 2) * 1536
                        pts = []
                        for ki, (kind, nk, kref, vref, bmi) in enumerate(kts):
                            bk = bank()
                            ofull = ps.r(bk, bk + 4 * nq, 0, nk)
                            bmfull = bm.r(bmi * 1024 + hk * 512, bmi * 1024 + hk * 512 + 4 * nq, 0, nk)
                            idr = identb.r(0, nk, 0, nk)
                            fns = [lambda e, o=ofull, idr=idr, bmr=bmfull: e.matmul(o.ap, idr.ap, bmr.ap, start=True, stop=False)]
                            rds = [idr, bmfull]
                            for hf in range(2):
                                o = ps.r(bk + hf * 2 * nq, bk + (hf + 1) * 2 * nq, 0, nk)
                                kr = kref(hk * 2 + hf)
                                qr = ar.r3(o_q + (2 * hk) * TM + t0, 2, TM, nq)
                                fns.append(lambda e, o=o, kr=kr, qr=qr, hf=hf: e.matmul(o.ap, kr.ap, qr.ap, start=False, stop=(hf == 1)))
                                rds += [kr, qr]
                            P.pe_group(fns, rds, [ofull])
                            pr = ps.r(bk, bk + 4 * nq, 0, nk)
                            pp = pt.r(pbase + ki * 512, pbase + ki * 512 + 4 * nq, 0, nk)
                            P.op("act", lambda e, pp=pp, pr=pr: e.activation(pp.ap, pr.ap, AF.Exp), [pr], [pp])
                            pts.append((pbase + ki * 512, nk, vref))
                        return pts

                    def att_p2(idx, t0, nq, hk, pts):
                        bo = bank()
                        prs_o, prs_d = [], []
                        for (pb, nk, vref) in pts:
                            for par in range(2):
                                rhs = pt.r(pb + par * 2 * nq, pb + (par + 1) * 2 * nq, 0, nk)
                                prs_o.append((vref(hk, par), rhs))
                                oz = onesz.r(64 - 64 * par, 192 - 64 * par, 0, nk)
                                prs_d.append((oz, rhs))
                        O3 = ps.r3(bo, 2, nq, nq)
                        D3 = ps.r3(bo + 256, 2, nq, nq)
                        mm(O3, prs_o, "C")
                        mm(D3, prs_d, "C")
                        ds = dsb.r3((idx % 2) * 256, 2, nq, nq)
                        es = esink.r3(l * 512 + hk * 256, 2, 128, nq)
                        P.op("dve", lambda e, ds=ds, D3=D3, es=es: e.tensor_tensor(ds.ap, D3.ap, es.ap, ALU.add), [D3, es], [ds])
                        P.op("dve", lambda e, ds=ds: e.reciprocal(ds.ap, ds.ap), [ds], [ds])
                        ao = ar.r3(o_a + (2 * hk) * TM + t0, 2, TM, nq)
                        P.op("dve", lambda e, ao=ao, O3=O3, ds=ds: e.tensor_tensor(ao.ap, O3.ap, ds.ap, ALU.mult), [O3, ds], [ao])

                    prev = None
                    cl_at = max(len(att_items) - 2, 0)
                    for idx, (t0, nq, kts, hk) in enumerate(att_items):
                        pts = att_p1(idx, t0, nq, kts, hk)
                        if prev is not None:
                            att_p2(*prev)
                        prev = (idx, t0, nq, hk, pts)
                        if idx == cl_at:
                            conv_ln()
                        if idx == len(att_items) - 1:
                            for fdef in cl_defer:
                                fdef()
                    att_p2(*prev)
                    if u + 1 < C.NU:
                        lt0 = Tu - 128
                        for v in range(4):
                            s_ = ar.r(o_kd + v * TM + lt0, o_kd + v * TM + lt0 + 128)
                            d_ = kcar.r(kc_base + v * 144 + 16, kc_base + v * 144 + 144)
                            P.op("pool", lambda e, s_=s_, d_=d_: e.tensor_copy(d_.ap, s_.ap), [s_], [d_])
                        lb = len(blocks) - 1
                        s_ = ar.r(o_vt + lb * 384, o_vt + lb * 384 + 384)
                        d_ = vcar.r(vc_base + 384, vc_base + 768)
                        P.op("pool", lambda e, s_=s_, d_=d_: e.tensor_copy(d_.ap, s_.ap), [s_], [d_])

                    for c0 in range(0, D, 512):
                        ncol = min(512, D - c0)
                        wga = load_w("w_in", l, 0, KD, 768 + 2 * C.CW + c0, ncol)
                        wgc = load_w("w_in", l, 0, KD, 768 + 2 * C.CW + D + c0, ncol)
                        wap = load_w("w_ap", l, 0, 4, c0, ncol)
                        wcp = load_w("w_cp", l, 0, KC, c0, ncol)
                        for mm_ in range(ncol // 128):
                            m = c0 // 128 + mm_
                            ca, cb = mm_ * 128, (mm_ + 1) * 128
                            for (t0, n) in tiles:
                                b1, b2, b3, b4 = bank(), bank(), bank(), bank()
                                pya, pyc, pga, pgc = (ps.r(b, b + n) for b in (b1, b2, b3, b4))
                                mm(pga, [(wga(k, ca, cb), hbr(k, t0, n)) for k in range(KD)], "D")
                                mm(pgc, [(wgc(k, ca, cb), hbr(k, t0, n)) for k in range(KD)], "D")
                                mm(pya, [(wap(k, ca, cb), ar.r(o_a + k * TM + t0, o_a + k * TM + t0 + n)) for k in range(4)], "D")
                                mm(pyc, [(wcp(k, ca, cb), ar.r(o_cs + k * TM + t0, o_cs + k * TM + t0 + n)) for k in range(KC)], "D")
                                dpar[0] ^= 1
                                if dpar[0]:
                                    sa, sc = tmpA.r(0, n), tmpB.r(0, n)
                                else:
                                    sa, sc = glb.r(0, n), glb.r(512, 512 + n)
                                t1, t2 = sa, sc
                                ba_, bc_ = pcol(l, C.c_bga + m), pcol(l, C.c_bgc + m)
                                P.op("act", lambda e, sa=sa, pga=pga, ba_=ba_: e.activation(sa.ap, pga.ap, AF.Sigmoid, bias=ba_.ap, scale=1.0), [pga, ba_], [sa])
                                P.op("act", lambda e, sc=sc, pgc=pgc, bc_=bc_: e.activation(sc.ap, pgc.ap, AF.Sigmoid, bias=bc_.ap, scale=1.0), [pgc, bc_], [sc])
                                P.op("dve", lambda e, t1=t1, pya=pya, sa=sa: e.tensor_tensor(t1.ap, pya.ap, sa.ap, ALU.mult), [pya, sa], [t1])
                                P.op("dve", lambda e, t2=t2, pyc=pyc, sc=sc: e.tensor_tensor(t2.ap, pyc.ap, sc.ap, ALU.mult), [pyc, sc], [t2])
                                o = ar.r(m * TM + t0, m * TM + t0 + n)
                                P.op("dve", lambda e, o=o, t1=t1, t2=t2: e.tensor_tensor(o.ap, t1.ap, t2.ap, ALU.add), [t1, t2], [o])
                    for c0 in range(0, D, 512):
                        ncol = min(512, D - c0)
                        wo = load_w("w_out", l, 0, KD, c0, ncol)
                        for mm_ in range(ncol // 128):
                            m = c0 // 128 + mm_
                            for (t0, n) in tiles:
                                bk = bank()
                                pr = ps.r(bk, bk + n)
                                mm(pr, [(wo(k, mm_ * 128, (mm_ + 1) * 128), ar.r(k * TM + t0, k * TM + t0 + n)) for k in range(KD)], "E")
                                hr = h.r(m * TM + t0, m * TM + t0 + n)
                                P.op("dve", lambda e, hr=hr, pr=pr: e.scalar_tensor_tensor(hr.ap, hr.ap, C.alpha, pr.ap, ALU.mult, ALU.add), [hr, pr], [hr])
                                ln_pre(h, TM, m, t0, n)
                    hst[0] = ln_stats_part1(KD, onesD.r(0, 128), tiles)
                    for (t0, n) in tiles:
                        h_ln(lambda c: pcol(l, C.c_l1g + c), lambda c: pcol(l, C.c_l1b + c), t0, n, hst[0])

                    uc_base = l * 2 * KF * 2
                    UBW = 16 + TM
                    wcur = {}

                    def ffn_a(j):
                        j0, jj = (j // 4) * 4, j % 4
                        if jj == 0:
                            nj = min(4, KF - j0)
                            wcur["u"] = load_w("w_up", l, 0, KD, j0 * 128, nj * 128)
                            wcur["g"] = load_w("w_up", l, 0, KD, C.DFF + j0 * 128, nj * 128)
                        wu, wg = wcur["u"], wcur["g"]
                        ob = (j % 2) * UBW
                        for (xb, which) in ((ub, 0), (gb, 1)):
                            hz = xb.r(ob + 14, ob + 16)
                            cr = ucar.r(uc_base + (which * KF + j) * 2, uc_base + (which * KF + j) * 2 + 2)
                            if u == 0:
                                P.op("pool", lambda e, hz=hz: e.memset(hz.ap, 0.0), [], [hz])
                            else:
                                P.op("pool", lambda e, hz=hz, cr=cr: e.tensor_copy(hz.ap, cr.ap), [cr], [hz])
                        for (t0, n) in tiles:
                            b1, b2 = bank(), bank()
                            p1, p2 = ps.r(b1, b1 + n), ps.r(b2, b2 + n)
                            mm(p1, [(wu(k, jj * 128, (jj + 1) * 128), hbr(k, t0, n)) for k in range(KD)], "U")
                            mm(p2, [(wg(k, jj * 128, (jj + 1) * 128), hbr(k, t0, n)) for k in range(KD)], "U")
                            o1, o2 = ub.r(ob + 16 + t0, ob + 16 + t0 + n), gb.r(ob + 16 + t0, ob + 16 + t0 + n)
                            P.op("act", lambda e, o1=o1, p1=p1: e.activation(o1.ap, p1.ap, AF.Identity), [p1], [o1])
                            P.op("dve", lambda e, o2=o2, p2=p2: e.tensor_copy(o2.ap, p2.ap), [p2], [o2])
                        if u + 1 < C.NU:
                            for (xb, which) in ((ub, 0), (gb, 1)):
                                s_ = xb.r(ob + 16 + Tu - 2, ob + 16 + Tu)
                                cr = ucar.r(uc_base + (which * KF + j) * 2, uc_base + (which * KF + j) * 2 + 2)
                                P.op("pool", lambda e, s_=s_, cr=cr: e.tensor_copy(cr.ap, s_.ap), [s_], [cr])

                    def ffn_b0(j):
                        dbase = (j % 2) * 32 * 128
                        for which in (0, 1):
                            for tap in range(3):
                                dgr = dg.r(dbase + (which * 3 + tap) * 128, dbase + (which * 3 + tap + 1) * 128)
                                wc_ = pcol(l, C.c_fdw + tap * 2 * KF + which * KF + j)
                                P.op("dve", lambda e, dgr=dgr, wc_=wc_: e.tensor_scalar(dgr.ap, identb.r(0, 128).ap, wc_.ap, None, ALU.mult),
                                     [identb.r(0, 128), wc_], [dgr])

                    def ffn_b(j):
                        ob = (j % 2) * UBW
                        cbk = [[bank() for _ in tiles] for _ in range(2)]
                        dbase = (j % 2) * 32 * 128
                        for which, xb in ((0, ub), (1, gb)):
                            for ti, (t0, n) in enumerate(tiles):
                                pr = ps.r(cbk[which][ti], cbk[which][ti] + n)
                                prs = []
                                for tap in range(3):
                                    dgr = dg.r(dbase + (which * 3 + tap) * 128, dbase + (which * 3 + tap + 1) * 128)
                                    prs.append((dgr, xb.r(ob + 14 + tap + t0, ob + 14 + tap + t0 + n)))
                                mm(pr, prs, "V")
                        for ti, (t0, n) in enumerate(tiles):
                            pu = ps.r(cbk[0][ti], cbk[0][ti] + n)
                            pg = ps.r(cbk[1][ti], cbk[1][ti] + n)
                            gl = glb.r((j % 2) * 512, (j % 2) * 512 + n)
                            bgc_, buc_ = pcol(l, C.c_fdb + KF + j), pcol(l, C.c_fdb + j)
                            P.op("act", lambda e, gl=gl, pg=pg, bgc_=bgc_: e.activation(gl.ap, pg.ap, AF.Gelu, bias=bgc_.ap, scale=1.0), [pg, bgc_], [gl])
                            o = ar.r(j * TM + t0, j * TM + t0 + n)
                            P.op("dve", lambda e, o=o, pu=pu, buc_=buc_, gl=gl: e.scalar_tensor_tensor(
                                o.ap, pu.ap, buc_.ap, gl.ap, ALU.add, ALU.mult), [pu, buc_, gl], [o])

                    ffn_b0(0)
                    ffn_a(0)
                    for j in range(KF):
                        if j + 1 < KF:
                            ffn_b0(j + 1)
                            ffn_a(j + 1)
                        ffn_b(j)
                    kgs = [(k0, min(8, KF - k0)) for k0 in range(0, KF, 8)]
                    assert len(kgs) <= NSLOT - 1
                    for c0 in range(0, D, 512):
                        ncol = min(512, D - c0)
                        wds = [load_w("w_dn", l, k0 * 128, kn, c0, ncol) for (k0, kn) in kgs]
                        for mm_ in range(ncol // 128):
                            m = c0 // 128 + mm_
                            for (t0, n) in tiles:
                                bk = bank()
                                pr = ps.r(bk, bk + n)
                                prs = []
                                for gi, (k0, kn) in enumerate(kgs):
                                    for k in range(kn):
                                        prs.append((wds[gi](k, mm_ * 128, (mm_ + 1) * 128), ar.r((k0 + k) * TM + t0, (k0 + k) * TM + t0 + n)))
                                mm(pr, prs, "W")
                                hr = h.r(m * TM + t0, m * TM + t0 + n)
                                P.op("dve", lambda e, hr=hr, pr=pr: e.scalar_tensor_tensor(hr.ap, hr.ap, C.alpha, pr.ap, ALU.mult, ALU.add), [hr, pr], [hr])
                                ln_pre(h, TM, m, t0, n)
                    hst[0] = ln_stats_part1(KD, onesD.r(0, 128), tiles)
                    for (t0, n) in tiles:
                        h_ln(lambda c: pcol(l, C.c_l2g + c), lambda c: pcol(l, C.c_l2b + c), t0, n, hst[0], need_hb=(l + 1 < (L if dbg is None else dbg)))
                    zr = ar.r(o_vt, o_vt + (C.NB + 1) * 384)
                    P.op("pool", lambda e, zr=zr: e.memset(zr.ap, 0.0), [], [zr])

                for (t0, n, src) in blocks:
                    if src == "meta":
                        continue
                    io_i[0] ^= 1
                    xo = io_i[0] * D
                    for c0 in range(0, KD, 4):
                        bk = bank()
                        nch = min(4, KD - c0)
                        for c in range(c0, c0 + nch):
                            o = ps.r(bk + (c - c0) * 128, bk + (c - c0 + 1) * 128, 0, n)
                            i_ = h.r(c * TM + t0, c * TM + t0 + n)
                            idr = ident.r(0, 128)
                            P.pe_group([lambda e, o=o, i_=i_, idr=idr: e.transpose(o.ap, i_.ap, idr.ap)], [i_, idr], [o])
                        pr = ps.r(bk, bk + nch * 128, 0, n)
                        orr = iob.r(xo + c0 * 128, xo + (c0 + nch) * 128, 0, n)
                        P.op("act", lambda e, orr=orr, pr=pr: e.activation(orr.ap, pr.ap, AF.Identity), [pr], [orr])
                    orr = iob.r(xo, xo + D, 0, n)
                    dst = out_d[sq, src:src + n, :]
                    P.dma("sp", lambda e, orr=orr, dst=dst: e.dma_start(out=dst, in_=orr.ap), [iob.r(xo, xo + D)], [], t_os[io_i[0]])
        P.finish(t_os)
        plan_out = {"descs": wl["plan"], "last_use": dict(P.last_use)}
    return nc, plan_out


class _CSBuf:
    def __init__(s, buf, base):
        s.buf, s.base = buf, base

    def r(s, a, b, p0=0, p1=128):
        return s.buf.r(s.base + a, s.base + b, p0, p1)


def _t5_bucket(d):
    d = np.asarray(d)
    n = np.maximum(d, 0)
    nf = np.maximum(n, 1).astype(np.float32)
    large = 16 + (np.log(nf / np.float32(16)) / np.float32(math.log(128 / 16)) * np.float32(16)).astype(np.int32)
    large = np.minimum(large, 31)
    return np.where(n < 16, n, large)


def _bias_tables(rel_bias):
    bg = np.zeros((5, 128, 2, 4, 128), np.float32)
    mk = np.zeros((5, 128, 2, 4, 128), np.float32)
    k = np.arange(128)[:, None]
    q = np.arange(128)[None, :]
    heads = (np.arange(2)[:, None] * 4 + np.array([0, 2, 1, 3])[None, :])
    def put(i, dist, valid, rows, ncols):
        b = _t5_bucket(dist)
        g = rel_bias[b][:, :, heads]
        bg[i, :rows, :, :, :ncols] = np.transpose(g, (0, 2, 3, 1))
        m = np.where(valid, 0.0, NEG).astype(np.float32)
        mk[i, :rows, :, :, :ncols] = m[:, None, None, :]
    put(0, q - k, q >= k, 128, 128)
    put(1, 128 + q - k, k > q, 128, 128)
    m16 = np.arange(16)[:, None]
    put(2, 16 + q - m16, np.ones((16, 128), bool), 16, 128)
    put(3, 16 + 128 + q - m16, np.ones((16, 128), bool), 16, 128)
    q16 = np.arange(16)[None, :]
    b = _t5_bucket(q16 - m16)
    g = rel_bias[b][:, :, heads]
    t = np.transpose(g, (0, 2, 3, 1))
    m = np.where(q16 >= m16, 0.0, NEG).astype(np.float32)
    bg4 = np.zeros((128, 2, 512), np.float32)
    mk4 = np.zeros((128, 2, 512), np.float32)
    bg4[:16, :, :64] = t.reshape(16, 2, 64)
    mk4[:16, :, :64] = np.broadcast_to(m[:, None, None, :], (16, 2, 4, 16)).reshape(16, 2, 64)
    bgf = bg.reshape(5, 128, 1024).copy()
    mkf = mk.reshape(5, 128, 1024).copy()
    bgf[4] = bg4.reshape(128, 1024)
    mkf[4] = mk4.reshape(128, 1024)
    return (np.ascontiguousarray(np.transpose(bgf, (1, 0, 2)).reshape(128, 5 * 1024)),
            np.ascontiguousarray(np.transpose(mkf, (1, 0, 2)).reshape(128, 5 * 1024)))


def _cols(v):
    v = np.asarray(v, np.float32)
    return np.ascontiguousarray(v.reshape(-1, 128).T)


def prep_inputs(C, inp):
    L, KD, KC, KF = C.L, C.KD, C.KC, C.KF
    f = lambda a: np.ascontiguousarray(np.asarray(a, np.float32))
    pvec = np.zeros((L, 128, C.NV), np.float32)
    z64 = np.zeros(64, np.float32)
    CW, D = C.CW, C.D
    for l in range(L):
        b = f(inp["b_in"][l])
        pv = pvec[l]
        pv[:, C.c_bq:C.c_bq + 4] = _cols(b[0:512])
        for hk in range(2):
            bk = b[512 + hk * 64: 512 + hk * 64 + 64]
            pv[:, C.c_bk + hk * 2 + 0] = np.concatenate([bk, z64])
            pv[:, C.c_bk + hk * 2 + 1] = np.concatenate([z64, bk])
        pv[:, C.c_bcv:C.c_bcv + KC] = _cols(b[768:768 + CW])
        pv[:, C.c_bcg:C.c_bcg + KC] = _cols(b[768 + CW:768 + 2 * CW])
        pv[:, C.c_bga:C.c_bga + KD] = _cols(b[768 + 2 * CW:768 + 2 * CW + D])
        pv[:, C.c_bgc:C.c_bgc + KD] = _cols(b[768 + 2 * CW + D:768 + 2 * CW + 2 * D])
        cdw = f(inp["conv_dw"][l])
        for tap in range(31):
            pv[:, C.c_cdw + tap * KC: C.c_cdw + (tap + 1) * KC] = _cols(cdw[tap])
        pv[:, C.c_cdb:C.c_cdb + KC] = _cols(inp["conv_dw_b"][l])
        pv[:, C.c_clg:C.c_clg + KC] = _cols(inp["conv_ln_g"][l])
        pv[:, C.c_clb:C.c_clb + KC] = _cols(inp["conv_ln_b"][l])
        pv[:, C.c_l1g:C.c_l1g + KD] = _cols(inp["ln1_g"][l])
        pv[:, C.c_l1b:C.c_l1b + KD] = _cols(inp["ln1_b"][l])
        fdw = f(inp["ffn_dw"][l])
        for tap in range(3):
            pv[:, C.c_fdw + tap * 2 * KF: C.c_fdw + (tap + 1) * 2 * KF] = _cols(fdw[tap])
        pv[:, C.c_fdb:C.c_fdb + 2 * KF] = _cols(inp["ffn_dw_b"][l])
        pv[:, C.c_l2g:C.c_l2g + KD] = _cols(inp["ln2_g"][l])
        pv[:, C.c_l2b:C.c_l2b + KD] = _cols(inp["ln2_b"][l])
    pv0 = np.concatenate([_cols(inp["in_ln_g"]), _cols(inp["in_ln_b"])], axis=1)
    bv = np.stack([np.broadcast_to(f(inp["b_in"][l])[640:768][None, :], (128, 128)) for l in range(L)])
    sinks = f(inp["attn_sinks"])
    p = np.arange(128)
    sl = np.zeros((L, 128, 2, 2, 128), np.float32)
    for l in range(L):
        for hk in range(2):
            for gp in range(2):
                hd = 4 * hk + 2 * gp + (p >= 64).astype(np.int64)
                sl[l, :, hk, gp, :] = sinks[l][hd][:, None]
    bg, mk = _bias_tables(f(inp["rel_bias"]))
    shared = {
        "meta": f(inp["meta_tokens"]), "w_in": f(inp["w_in"]), "w_ap": f(inp["w_attn_proj"]),
        "w_cp": f(inp["w_conv_proj"]), "w_out": f(inp["w_out"]), "w_up": f(inp["ffn_w_up"]),
        "w_dn": f(inp["ffn_w_down"]), "pvec": pvec, "pv0": np.ascontiguousarray(pv0),
        "bv": np.ascontiguousarray(bv), "sinkl": np.ascontiguousarray(sl.reshape(L, 128, 512)),
        "bias_g": bg, "mask_c": mk, "ident": np.eye(128, dtype=np.float32),
    }
    return shared


def kernel(**inputs):
    C = Cfg()
    x = np.ascontiguousarray(np.asarray(inputs["x"], np.float32))
    shared = prep_inputs(C, inputs)
    ncores = 8
    _, plan = build(C)
    nc, _ = build(C, plan=plan)
    in_maps = []
    for i in range(ncores):
        m = dict(shared)
        m["x"] = np.ascontiguousarray(x[i * C.NSEQ:(i + 1) * C.NSEQ])
        in_maps.append(m)
    res = run_bass_kernel_spmd(nc, in_maps, core_ids=list(range(ncores)))
    out = np.concatenate([np.asarray(r["out"]) for r in res.results], axis=0)
    return out.astype(np.float32)
```
